# Optimizing a Trainium2 kernel written in Bass

```python
import jax, jax.numpy as jnp
from jax import lax
import numpy as np

D_MODEL = 2048
BATCH = 32
SEQ = 256
DEPTH = 1
DEC_BATCH = 4
DEC_SEQ = 1024
PAST_LEN = 256

GRID_W = 64
GLA_HEADS = 4
GLA_DK = 128
GLA_DV = 256
GLA_KEY_WIDTH = GLA_HEADS * GLA_DK
GLA_VAL_WIDTH = GLA_HEADS * GLA_DV
GLA_RANK = 16
GLA_TAU = 16.0
FOURIER_GROUPS = 4
FOURIER_GROUP_DIM = 256
FOURIER_WIDTH = FOURIER_GROUPS * FOURIER_GROUP_DIM
IN_COLS = 2 * GLA_KEY_WIDTH + 2 * GLA_VAL_WIDTH + 2 * GLA_RANK + FOURIER_WIDTH
FFN_HIDDEN = ((8 * D_MODEL + 3 * 256 - 1) // (3 * 256)) * 256
N_MOD = 6
EPS = 1e-6

kernel_name = "hybrid_gla_fnet_diffusion_step"


def rmsnorm(x, g):
    xf = x.astype(jnp.float32)
    y = xf * lax.rsqrt(jnp.mean(xf * xf, axis=-1, keepdims=True) + EPS)
    return (y * g.astype(jnp.float32)).astype(x.dtype)


def gla_chunked(q, k, v, log_a, s0, n_chunks):
    bsz, nh, t, dk = q.shape
    dv = v.shape[-1]
    c = t // n_chunks
    qc = q.reshape(bsz, nh, n_chunks, c, dk)
    kc = k.reshape(bsz, nh, n_chunks, c, dk)
    vc = v.reshape(bsz, nh, n_chunks, c, dv)
    lc = log_a.reshape(bsz, nh, n_chunks, c, dk)
    b = jnp.cumsum(lc, axis=3)
    total = b[:, :, :, -1:, :]
    q_in = qc * jnp.exp(b)
    k_in = kc * jnp.exp(-b)
    k_out = kc * jnp.exp(total - b)
    mask = jnp.tril(jnp.ones((c, c), dtype=bool))
    att = jnp.where(mask, jnp.einsum('bhnid,bhnjd->bhnij', q_in, k_in), 0.0)
    o_intra = jnp.einsum('bhnij,bhnjv->bhniv', att, vc)
    decay = jnp.exp(total[:, :, :, 0, :])

    def step(state, xs):
        qi, ko, vi, dec = xs
        o = jnp.einsum('bhid,bhdv->bhiv', qi, state)
        state = state * dec[..., None] + jnp.einsum('bhid,bhiv->bhdv', ko, vi)
        return state, o

    xs = (jnp.moveaxis(q_in, 2, 0), jnp.moveaxis(k_out, 2, 0),
          jnp.moveaxis(vc, 2, 0), jnp.moveaxis(decay, 2, 0))
    s_fin, o_inter = lax.scan(step, s0, xs)
    o = o_intra + jnp.moveaxis(o_inter, 0, 2)
    return o.reshape(bsz, nh, t, dv), s_fin


def mixer(h, s0_f, s0_b, n_chunks, lw):
    bsz, t, _ = h.shape
    proj = h @ lw['w_in']
    splits = np.cumsum([GLA_KEY_WIDTH, GLA_KEY_WIDTH, GLA_VAL_WIDTH, GLA_VAL_WIDTH,
                        GLA_RANK, GLA_RANK]).tolist()
    q, k, v, g, lr_f, lr_b, fo = jnp.split(proj, splits, axis=-1)

    def heads(z, d):
        return z.astype(jnp.float32).reshape(bsz, t, GLA_HEADS, d).transpose(0, 2, 1, 3)

    qh = heads(q, GLA_DK) * (GLA_DK ** -0.5)
    kh = heads(k, GLA_DK)
    vh = heads(v, GLA_DV)
    la_f = heads(jax.nn.log_sigmoid((lr_f @ lw['w_a2_fwd'] + lw['b_a_fwd']).astype(jnp.float32)) / GLA_TAU, GLA_DK)
    la_b = heads(jax.nn.log_sigmoid((lr_b @ lw['w_a2_bwd'] + lw['b_a_bwd']).astype(jnp.float32)) / GLA_TAU, GLA_DK)
    o_f, s_f = gla_chunked(qh, kh, vh, la_f, s0_f.astype(jnp.float32), n_chunks)
    flip = lambda z: jnp.flip(z, axis=2)
    o_b, s_b = gla_chunked(flip(qh), flip(kh), flip(vh), flip(la_b), s0_b.astype(jnp.float32), n_chunks)
    o = o_f + flip(o_b)
    o = o * lax.rsqrt(jnp.mean(o * o, axis=-1, keepdims=True) + EPS)
    o = o * lw['gla_out_norm'].astype(jnp.float32)[None, :, None, :]
    o = o.transpose(0, 2, 1, 3).reshape(bsz, t, GLA_VAL_WIDTH)
    o = (o * jax.nn.silu(g.astype(jnp.float32))).astype(h.dtype)

    fg = fo.astype(jnp.float32).reshape(bsz, t, FOURIER_GROUPS, FOURIER_GROUP_DIM)
    f = jnp.fft.fft2(fg, axes=(1, 3), norm='ortho').real
    f = f.reshape(bsz, t, FOURIER_WIDTH).astype(h.dtype)

    gates = jax.nn.sigmoid(h @ lw['w_gate'] + lw['b_gate'])
    g_a, g_b = jnp.split(gates, 2, axis=-1)
    merged = g_a * (o @ lw['w_br_gla']) + g_b * (f @ lw['w_br_four'])
    return merged @ lw['w_out'], s_f, s_b


def layer(x, cond, s0_f, s0_b, n_chunks, lw):
    mod = jax.nn.silu(cond) @ lw['w_ada'] + lw['b_ada']
    sh1, sc1, ga1, sh2, sc2, ga2 = [m[:, None, :] for m in jnp.split(mod, N_MOD, axis=-1)]
    h = rmsnorm(x, lw['norm_pre_mix']) * (1.0 + sc1) + sh1
    m, s_f, s_b = mixer(h, s0_f, s0_b, n_chunks, lw)
    x = x + ga1 * rmsnorm(m, lw['norm_post_mix'])
    h = rmsnorm(x, lw['norm_pre_ffn']) * (1.0 + sc2) + sh2
    ff = (jax.nn.silu(h @ lw['w_ffn_gate']) * (h @ lw['w_ffn_up'])) @ lw['w_ffn_down']
    x = x + ga2 * rmsnorm(ff, lw['norm_post_ffn'])
    return x, s_f, s_b


def setup_inputs(seed: int = 0) -> dict:
    key = jax.random.key(seed)
    ks = jax.random.split(key, 32)
    f32 = jnp.float32
    nrm = lambda i, shape, s: (jax.random.normal(ks[i], shape, f32) * s)
    gain = lambda i, shape: 1.0 + 0.05 * jax.random.normal(ks[i], shape, f32)
    st_shape = (DEC_BATCH, DEPTH, GLA_HEADS, GLA_DK, GLA_DV)
    return {
        'x_prompt': nrm(0, (BATCH, SEQ, D_MODEL), 1.0),
        'x_sample': nrm(1, (DEC_BATCH, DEC_SEQ, D_MODEL), 1.0),
        'state_gla_fwd': nrm(2, st_shape, 1.0),
        'state_gla_bwd': nrm(3, st_shape, 1.0),
        'c': nrm(4, (DEC_BATCH, D_MODEL), 1.0),
        'c_ctx': nrm(5, (D_MODEL,), 1.0),
        'w_ada': nrm(6, (DEPTH, D_MODEL, N_MOD * D_MODEL), D_MODEL ** -0.5),
        'b_ada': nrm(7, (DEPTH, N_MOD * D_MODEL), 0.02),
        'norm_pre_mix': gain(8, (DEPTH, D_MODEL)),
        'norm_post_mix': gain(9, (DEPTH, D_MODEL)),
        'norm_pre_ffn': gain(10, (DEPTH, D_MODEL)),
        'norm_post_ffn': gain(11, (DEPTH, D_MODEL)),
        'w_in': nrm(12, (DEPTH, D_MODEL, IN_COLS), D_MODEL ** -0.5),
        'w_a2_fwd': nrm(13, (DEPTH, GLA_RANK, GLA_KEY_WIDTH), GLA_RANK ** -0.5),
        'b_a_fwd': nrm(14, (DEPTH, GLA_KEY_WIDTH), 0.1),
        'w_a2_bwd': nrm(15, (DEPTH, GLA_RANK, GLA_KEY_WIDTH), GLA_RANK ** -0.5),
        'b_a_bwd': nrm(16, (DEPTH, GLA_KEY_WIDTH), 0.1),
        'gla_out_norm': gain(17, (DEPTH, GLA_HEADS, GLA_DV)),
        'w_br_gla': nrm(18, (DEPTH, GLA_VAL_WIDTH, D_MODEL), GLA_VAL_WIDTH ** -0.5),
        'w_br_four': nrm(19, (DEPTH, FOURIER_WIDTH, D_MODEL), FOURIER_WIDTH ** -0.5),
        'w_gate': nrm(20, (DEPTH, D_MODEL, 2 * D_MODEL), D_MODEL ** -0.5),
        'b_gate': nrm(21, (DEPTH, 2 * D_MODEL), 0.02),
        'w_out': nrm(22, (DEPTH, D_MODEL, D_MODEL), D_MODEL ** -0.5),
        'w_ffn_gate': nrm(23, (DEPTH, D_MODEL, FFN_HIDDEN), D_MODEL ** -0.5),
        'w_ffn_up': nrm(24, (DEPTH, D_MODEL, FFN_HIDDEN), D_MODEL ** -0.5),
        'w_ffn_down': nrm(25, (DEPTH, FFN_HIDDEN, D_MODEL), FFN_HIDDEN ** -0.5),
    }


def reference(x_prompt, x_sample, state_gla_fwd, state_gla_bwd, c, c_ctx, w_ada, b_ada,
              norm_pre_mix, norm_post_mix, norm_pre_ffn, norm_post_ffn, w_in,
              w_a2_fwd, b_a_fwd, w_a2_bwd, b_a_bwd, gla_out_norm, w_br_gla, w_br_four,
              w_gate, b_gate, w_out, w_ffn_gate, w_ffn_up, w_ffn_down):
    ctx_chunks = x_prompt.shape[1] // GRID_W
    rows = x_sample.shape[1] // GRID_W
    bp = x_prompt.shape[0]
    zero_state = jnp.zeros((bp, GLA_HEADS, GLA_DK, GLA_DV), jnp.float32)
    cond_ctx = c_ctx[None, :]
    xp, xs = x_prompt, x_sample
    new_f, new_b = [], []
    for l in range(DEPTH):
        lw = dict(w_ada=w_ada[l], b_ada=b_ada[l], norm_pre_mix=norm_pre_mix[l],
                  norm_post_mix=norm_post_mix[l], norm_pre_ffn=norm_pre_ffn[l],
                  norm_post_ffn=norm_post_ffn[l], w_in=w_in[l], w_a2_fwd=w_a2_fwd[l],
                  b_a_fwd=b_a_fwd[l], w_a2_bwd=w_a2_bwd[l], b_a_bwd=b_a_bwd[l],
                  gla_out_norm=gla_out_norm[l], w_br_gla=w_br_gla[l], w_br_four=w_br_four[l],
                  w_gate=w_gate[l], b_gate=b_gate[l], w_out=w_out[l],
                  w_ffn_gate=w_ffn_gate[l], w_ffn_up=w_ffn_up[l], w_ffn_down=w_ffn_down[l])
        xp, s_f, s_b = layer(xp, cond_ctx, zero_state, zero_state, ctx_chunks, lw)
        new_f.append(s_f)
        new_b.append(s_b)
        xs, _, _ = layer(xs, c, state_gla_fwd[:, l], state_gla_bwd[:, l], rows, lw)
    new_state_gla_fwd = jnp.stack(new_f, axis=1)
    new_state_gla_bwd = jnp.stack(new_b, axis=1)
    return (xp, xs, new_state_gla_fwd, new_state_gla_bwd)
```

```python
import os
import numpy as np
import ml_dtypes
import concourse.bass as bass
import concourse.mybir as mybir
import concourse.bass_utils as bass_utils

F32 = mybir.dt.float32
BF16 = mybir.dt.bfloat16
AF = mybir.ActivationFunctionType
ALU = mybir.AluOpType

D = 2048
NTOK = 1536
NT = 12
HID = 5632
EPS = 1e-6
IN_COLS = 4128
PASSES = [(0, 8), (8, 4)]


class Ev:
    __slots__ = ("sem", "val")

    def __init__(self, sem, val):
        self.sem = sem
        self.val = val


class Buf:
    __slots__ = ("w", "r", "name", "excl")

    def __init__(self, name="", excl=False):
        self.w = []
        self.r = []
        self.name = name
        self.excl = excl


class DmaSem:
    def __init__(self, nc, name):
        self.sem = nc.alloc_semaphore(name=name)
        self.count = 0
        self.in_barrier = True


class Prog:
    ENGS = ("pe", "act", "dve", "pool", "sp")

    def __init__(self, nc):
        self.nc = nc
        self.ops = {e: [] for e in self.ENGS}
        self.sem = {e: nc.alloc_semaphore(name=f"q_{e}") for e in self.ENGS}
        self.cnt = {e: 0 for e in self.ENGS}
        self.nsem = 0
        self.nops = 0
        self.dsems = []
        self.marks = []

    def dsem(self, name):
        self.nsem += 1
        d = DmaSem(self.nc, f"d_{name}_{self.nsem}")
        self.dsems.append(d)
        return d

    def mark(self, name):
        self.marks.append((name, dict(self.cnt)))

    def barrier(self, skip=()):
        evs = [Ev(self.sem[x], self.cnt[x]) for x in self.ENGS if self.cnt[x] > 0]
        evs += [Ev(d.sem, d.count) for d in self.dsems if d.count > 0 and d.in_barrier and d not in skip]
        for e in self.ENGS:
            self.ops[e].append((None, list(evs), None, 0))

    def _deps(self, reads, writes, wacc, extra):
        waits = list(extra)
        for b in reads:
            waits += b.w
            if b.excl:
                waits += b.r
        for b in list(writes) + list(wacc):
            waits += b.w
            waits += b.r
        return [w for w in waits if w is not None]

    def _update(self, ev, reads, writes, wacc):
        for b in reads:
            b.r.append(ev)
        for b in writes:
            b.w = [ev]
            b.r = []
        for b in wacc:
            b.w.append(ev)

    def op(self, eng, fn, reads=(), writes=(), wacc=(), extra=()):
        waits = self._deps(reads, writes, wacc, extra)
        self.cnt[eng] += 1
        ev = Ev(self.sem[eng], self.cnt[eng])
        self.ops[eng].append((fn, waits, ev, 1))
        self._update(ev, reads, writes, wacc)
        self.nops += 1
        return ev

    def dma(self, eng, out_ap, in_ap, dsem, reads=(), writes=(), wacc=(), extra=()):
        waits = self._deps(reads, writes, wacc, extra)
        dsem.count += 16
        ev = Ev(dsem.sem, dsem.count)
        fn = (lambda e, o=out_ap, i=in_ap: e.dma_start(out=o, in_=i))
        self.ops[eng].append((fn, waits, ev, 16))
        self._update(ev, reads, writes, wacc)
        return ev

    def wait_only(self, eng, waits):
        self.ops[eng].append((None, [w for w in waits if w is not None], None, 0))

    def emit(self):
        nc = self.nc
        with nc.Block() as block:
            def run(engname):
                def body(e):
                    waited = {}
                    for fn, waits, ev, inc in self.ops[engname]:
                        best = {}
                        for w in waits:
                            k = id(w.sem)
                            if waited.get(k, 0) >= w.val:
                                continue
                            if k not in best or best[k].val < w.val:
                                best[k] = w
                        for k, w in best.items():
                            waited[k] = w.val
                            e.wait_ge(w.sem, w.val)
                        if fn is None:
                            continue
                        ins = fn(e)
                        if ev is not None:
                            ins.then_inc(ev.sem, inc)
                return body
            block.tensor(run("pe"))
            block.scalar(run("act"))
            block.vector(run("dve"))
            block.gpsimd(run("pool"))
            block.sync(run("sp"))


class Arena:
    def __init__(self, nc, nbytes):
        self.nc = nc
        self.t = nc.alloc_sbuf_tensor("arena", [128, nbytes], mybir.dt.uint8)
        self.nbytes = nbytes

    def view(self, off, shape, dtype):
        esz = 2 if dtype == BF16 else 4
        n = 1
        for s in shape[1:]:
            n *= s
        assert off % 4 == 0 and off + n * esz <= self.nbytes, (off, shape, self.nbytes)
        ap = self.t[:, off:off + n * esz].bitcast(dtype)
        if len(shape) > 2:
            names = " ".join(f"d{i}" for i in range(len(shape) - 1))
            kw = {f"d{i}": shape[i + 1] for i in range(len(shape) - 1)}
            ap = ap.rearrange(f"p ({names}) -> p {names}", **kw)
        return ap


class Ring:
    def __init__(self, prog, arena, off, nslots, slot_bytes):
        self.p = prog
        self.arena = arena
        self.off = off
        self.n = nslots
        self.sb = slot_bytes
        self.sems = [prog.dsem(f"ring{i}") for i in range(nslots)]
        for d_ in self.sems:
            d_.in_barrier = False
        self.bufs = [Buf(f"ring{i}") for i in range(nslots)]
        self.pending = []
        self.keys = {}
        self.next_fill = 0
        self.released = 0

    def add(self, key, dmas):
        self.keys[key] = len(self.pending)
        self.pending.append(dmas)

    def slot_view(self, slot, shape, dtype, boff=0):
        return self.arena.view(self.off + slot * self.sb + boff, shape, dtype)

    def _pump(self):
        while self.next_fill < len(self.pending) and self.next_fill < self.released + self.n:
            idx = self.next_fill
            slot = idx % self.n
            first = True
            for (eng, mk) in self.pending[idx]:
                out_ap, in_ap = mk(slot)
                if first:
                    self.p.dma(eng, out_ap, in_ap, self.sems[slot], writes=[self.bufs[slot]])
                else:
                    self.p.dma(eng, out_ap, in_ap, self.sems[slot], wacc=[self.bufs[slot]])
                first = False
            self.next_fill += 1

    def acquire(self, key):
        idx = self.keys[key]
        self._pump()
        assert idx == self.released and idx < self.next_fill, (key, idx, self.next_fill, self.released)
        slot = idx % self.n
        return slot, self.bufs[slot]

    def acquire2(self, key):
        idx = self.keys[key]
        self._pump()
        assert idx == self.released + 1 and idx < self.next_fill, (key, idx, self.next_fill, self.released)
        slot = idx % self.n
        return slot, self.bufs[slot]

    def release(self, key):
        idx = self.keys[key]
        assert idx == self.released, (key, idx, self.released)
        self.released += 1
        self._pump()


def build_nc(stage=99, debug=None):
    nc = bass.Bass("TRN2", target_bir_lowering=False)
    KB = 1024
    SHAPES = dict(
        xin=([NTOK, D], F32), condl=([128, 32], F32), s0f=([4, 128, 256], F32), s0b=([4, 128, 256], F32),
        link=([128, 2], F32), ctA=([1024, 1024], BF16), nstA=([1024, 1024], BF16), ctB=([512, 512], BF16),
        nstB=([512, 512], BF16), cs=([256, 512], BF16), cum=([128, 2, 256], F32), amask=([128, 2, 128], F32),
        ident=([128, 128], BF16), w_ada=([D, 6 * D], F32), b_adal=([128, 96], F32), b_ada=([1, 6 * D], F32),
        gprel=([128, 32], F32), gpost=([2, D], F32), w_in=([D, IN_COLS], F32), a_fb=([33, 2, 512], F32),
        gn=([1, 1024], F32), w_br_gla=([1024, D], F32), w_br_four=([1024, D], F32), w_gate=([D, 2 * D], F32),
        b_gatel=([128, 32], F32), w_out=([D, D], F32), w_ffn_gate=([D, HID], F32), w_ffn_up=([D, HID], F32),
        w_ffn_down=([HID, D], F32))
    dts = {}

    def DI(name):
        if name not in dts:
            shp, dty = SHAPES[name]
            dts[name] = nc.dram_tensor(name, list(shp), dty, kind="ExternalInput").ap()
        return dts[name]

    def dout(name, shape, dtype=F32):
        return nc.dram_tensor(name, list(shape), dtype, kind="ExternalOutput").ap()

    yout = dout("yout", [NTOK, D])
    snf = dout("snf", [6, 4, 128, 256])
    snb = dout("snb", [6, 4, 128, 256])
    x1s = nc.dram_tensor("x1s", [NTOK, D], F32, kind="Internal").ap()
    dbg = {}
    if debug:
        for name, shape in debug.items():
            dbg[name] = dout("dbg_" + name, shape)

    P = Prog(nc)
    AR = Arena(nc, 206 * KB)
    PS = [nc.alloc_psum_tensor(f"psb{i}", [128, 512], F32) for i in range(8)]
    PSB = [Buf(f"psb{i}", excl=True) for i in range(8)]
    PS0h = [Buf("ps0a"), Buf("ps0b")]

    def psv(i, shape, dtype=F32, boff=0):
        esz = 2 if dtype == BF16 else 4
        n = 1
        for s in shape[1:]:
            n *= s
        assert boff % 4 == 0 and boff + n * esz <= 2048
        ap = PS[i][:, boff // 4:(boff + n * esz) // 4]
        if dtype == BF16:
            ap = ap.bitcast(BF16)
        if len(shape) > 2:
            names = " ".join(f"d{i}" for i in range(len(shape) - 1))
            kw = {f"d{i}": shape[i + 1] for i in range(len(shape) - 1)}
            ap = ap.rearrange(f"p ({names}) -> p {names}", **kw)
        return ap

    off = 0

    def take(n):
        nonlocal off
        o = off
        off += (n + 31) // 32 * 32
        return o
    ident_t = AR.view(take(256), [128, 128], BF16)
    amask_t = AR.view(take(1024), [128, 2, 128], F32)
    cum_t = AR.view(take(2048), [128, 2, 256], F32)
    cs_t = AR.view(take(2048), [128, 2, 512], BF16)
    condl_t = AR.view(take(128), [128, 32], F32)
    csb_t = AR.view(take(64), [128, 32], BF16)
    badal_t = AR.view(take(384), [128, 96], F32)
    gprel_t = AR.view(take(128), [128, 32], F32)
    bgatel_t = AR.view(take(128), [128, 32], F32)
    link_t = AR.view(take(8), [128, 2], F32)
    modF_t = AR.view(take(512), [128, 4, 16, 2], F32)
    stat_t = AR.view(take(4 * 256), [128, 256], F32)
    part_t = AR.view(take(4 * 96), [128, 96], F32)
    a_fb_t = AR.view(take(2 * 512 * 4), [128, 2, 512], F32)
    gag_t = AR.view(take(16 * KB), [128, 2, D], F32)
    O_RING = off
    RING_SLOT = 16 * KB
    NSLOT = 3
    off += NSLOT * RING_SLOT
    O_BIG = off
    BIGSZ = AR.nbytes - O_BIG
    print("arena const+ring bytes", off, "big region", BIGSZ)
    ring = Ring(P, AR, O_RING, NSLOT, RING_SLOT)

    sem_c = P.dsem("const")
    sem_m = P.dsem("mods")
    sem_m2 = P.dsem("mods2")
    B_const = Buf("const")
    B_stat = Buf("stat")

    for t_, nm in ((ident_t, "ident"), (amask_t, "amask"), (cum_t, "cum"), (condl_t, "condl"), (badal_t, "b_adal"),
                   (gprel_t, "gprel"), (bgatel_t, "b_gatel"), (link_t, "link")):
        P.dma("sp", t_, DI(nm), sem_c, wacc=[B_const])
    P.dma("sp", cs_t, DI("cs").rearrange("(c p) n -> p c n", p=128), sem_c, wacc=[B_const])
    P.dma("sp", a_fb_t[0:33], DI("a_fb"), sem_c, wacc=[B_const])
    P.op("dve", lambda e: e.memset(stat_t, 0.0), writes=[B_stat])
    B_part = Buf("part")
    P.op("dve", lambda e: e.memset(part_t, 0.0), writes=[B_part])
    B_csb = Buf("csb")
    P.op("act", lambda e: e.activation(out=csb_t, in_=condl_t, func=AF.Silu), reads=[B_const], writes=[B_csb])

    def wtile(src, r0, nk, c0, ncols, boff=0, tot=None, eng="pool"):
        tot_ = tot or ncols

        def mk(slot):
            o = ring.slot_view(slot, [128, nk, tot_], BF16)[:, :, boff:boff + ncols]
            i = src[r0:r0 + nk * 128, c0:c0 + ncols].rearrange("(k p) c -> p k c", p=128)
            return o, i
        return (eng, mk)

    def wdma(src_ap, view_fn, eng="pool"):
        def mk(slot):
            return view_fn(slot), src_ap
        return (eng, mk)

    def rows_pk(src, r0, nk, c0, ncols):
        return src[r0:r0 + nk * 128, c0:c0 + ncols].rearrange("(k p) c -> p k c", p=128)

    w_ada = DI("w_ada")
    for c in range(8 if stage >= 3 else 12):
        ring.add(("ada", c), [wtile(w_ada, 0, 16, c * 512, 512)])
    if stage >= 2:
        w_in = DI("w_in")
        for ps_ in range(2):
            ring.add(("lr", ps_), [wtile(w_in, 0, 16, 3072, 32)])
            def add_qk(h):
                ring.add(("qk", ps_, h), [wtile(w_in, 0, 16, h * 128, 128, 0, 256),
                                          wtile(w_in, 0, 16, 512 + h * 128, 128, 128, 256)])

            def add_vg(h):
                ring.add(("vg", ps_, h), [wtile(w_in, 0, 16, 1024 + h * 256, 256, 0, 512),
                                          wtile(w_in, 0, 16, 2048 + h * 256, 256, 256, 512)])
            add_qk(0)
            add_vg(0)
            for h in range(4):
                if h + 1 < 4:
                    add_vg(h + 1)
                if stage >= 4 and ps_ == 1 and h < 2:
                    for c in range(4):
                        ring.add(("ada", 12 + 4 * h + c), [wtile(w_ada, 0, 16, 6144 + (4 * h + c) * 512, 512)])
                if h + 1 < 4:
                    add_qk(h + 1)
            if stage >= 3 and ps_ == 0:
                for c in range(8, 12):
                    ring.add(("ada", c), [wtile(w_ada, 0, 16, c * 512, 512)])
            if stage >= 3:
                ntp_ = PASSES[ps_][1]
                T_ = ntp_ * 128
                ct_, nst_ = (DI("ctA"), DI("nstA")) if ps_ == 0 else (DI("ctB"), DI("nstB"))
                for g in range(4):
                    ring.add(("fo", ps_, g), [wtile(w_in, 0, 16, 3104 + g * 256, 256)])
                    for q5 in range(T_ // 512):
                        ring.add(("tab", ps_, g, q5), [
                            wdma(rows_pk(ct_, 0, ntp_, q5 * 512, 512),
                                 lambda slot, ntp_=ntp_: ring.slot_view(slot, [128, 2, ntp_, 512], BF16)[:, 0, :, :]),
                            wdma(rows_pk(nst_, 0, ntp_, q5 * 512, 512),
                                 lambda slot, ntp_=ntp_: ring.slot_view(slot, [128, 2, ntp_, 512], BF16)[:, 1, :, :])])
                for jp in range(8):
                    ring.add(("gate", ps_, jp), [
                        wdma(rows_pk(DI("w_gate"), 0, 16, jp * 256, 256),
                             lambda slot: ring.slot_view(slot, [128, 2, 16, 256], BF16)[:, 0, :, :]),
                        wdma(rows_pk(DI("w_gate"), 0, 16, 2048 + jp * 256, 256),
                             lambda slot: ring.slot_view(slot, [128, 2, 16, 256], BF16)[:, 1, :, :])])
                    ring.add(("br", ps_, jp), [
                        wdma(rows_pk(DI("w_br_gla"), 0, 8, jp * 256, 256),
                             lambda slot: ring.slot_view(slot, [128, 2, 8, 256], BF16)[:, 0, :, :]),
                        wdma(rows_pk(DI("w_br_four"), 0, 8, jp * 256, 256),
                             lambda slot: ring.slot_view(slot, [128, 2, 8, 256], BF16)[:, 1, :, :])])
                for c in range(4):
                    ring.add(("wo", ps_, c), [wtile(DI("w_out"), 0, 16, c * 512, 512)])
    if stage >= 4:
        for c in range(8, 12):
            ring.add(("ada", 12 + c), [wtile(w_ada, 0, 16, 6144 + c * 512, 512)])
        for fb in range(2):
            for hp in range(22):
                ring.add(("f1", fb, hp), [
                    wdma(rows_pk(DI("w_ffn_gate"), 0, 16, hp * 256, 256),
                         lambda slot: ring.slot_view(slot, [128, 2, 16, 256], BF16)[:, 0, :, :]),
                    wdma(rows_pk(DI("w_ffn_up"), 0, 16, hp * 256, 256),
                         lambda slot: ring.slot_view(slot, [128, 2, 16, 256], BF16)[:, 1, :, :])])
            for c in range(4):
                for pz in range(4):
                    ring.add(("f2", fb, c, pz), [wtile(DI("w_ffn_down"), pz * 1408, 11, c * 512, 512)])

    MT = {}

    def set_mods_tmp(base):
        MT["csrep"] = AR.view(base, [128, 16, 2, 128], BF16)
        MT["bga"] = AR.view(base + 8 * KB, [128, D], F32)
        MT["gpo"] = AR.view(base + 16 * KB, [128, D], F32)
    set_mods_tmp(O_BIG)
    B_csrep, B_bga, B_gpo, B_gag, B_modF = Buf(), Buf(), Buf(), Buf(), Buf()

    def mod_feature(key, mslot, csel):
        slot, wb = ring.acquire(key)
        wv = ring.slot_view(slot, [128, 16, 512], BF16)

        def fn(e):
            ins = None
            for q in range(4):
                ch = csel * 4 + q
                for k in range(16):
                    ins = e.matmul(psv(0, [128, 64, 2])[:, mslot * 16 + ch, :], lhsT=wv[:, k, q * 128:(q + 1) * 128],
                                   rhs=csb_t[:, 2 * k:2 * k + 2], start=(k == 0), stop=(k == 15))
            return ins
        P.op("pe", fn, reads=[wb, B_csb], wacc=[PSB[0]])
        ring.release(key)

    def evac_mod(m, ada_idx):
        P.op("dve", lambda e: e.tensor_tensor(
            out=modF_t[:, m, :, :], in0=psv(0, [128, 64, 2])[:, (m % 2) * 16:(m % 2 + 1) * 16, :],
            in1=badal_t[:, ada_idx * 16:(ada_idx + 1) * 16].unsqueeze(2).to_broadcast([128, 16, 2]),
            op=ALU.add), reads=[PSB[0], B_const], wacc=[B_modF])

    def mod_scale(m, goff):
        P.op("dve", lambda e: e.scalar_tensor_tensor(
            out=modF_t[:, m, :, :], in0=modF_t[:, m, :, :], scalar=1.0,
            in1=gprel_t[:, goff:goff + 16].unsqueeze(2).to_broadcast([128, 16, 2]),
            op0=ALU.add, op1=ALU.mult), reads=[B_const], writes=[B_modF])

    def mod_bcast(key, c):
        csrep_t, bga_t, gpo_t = MT["csrep"], MT["bga"], MT["gpo"]
        slot, wb = ring.acquire(key)
        wv = ring.slot_view(slot, [128, 16, 512], BF16)
        for j in range(2):
            def fn(e, j=j):
                ins = None
                for k in range(16):
                    ins = e.matmul(psv(1 + j, [128, 512]), lhsT=csrep_t[:, k, j, :], rhs=wv[:, k, :],
                                   start=(k == 0), stop=(k == 15))
                return ins
            P.op("pe", fn, reads=[wb, B_csrep], writes=[PSB[1 + j]])
        ring.release(key)
        for j in range(2):
            P.op("dve", lambda e, j=j: e.tensor_tensor(
                out=gag_t[:, j, c * 512:(c + 1) * 512], in0=psv(1 + j, [128, 512]),
                in1=bga_t[:, c * 512:(c + 1) * 512], op=ALU.add), reads=[PSB[1 + j], B_bga], wacc=[B_gag])
            P.op("dve", lambda e, j=j: e.tensor_tensor(
                out=gag_t[:, j, c * 512:(c + 1) * 512], in0=gag_t[:, j, c * 512:(c + 1) * 512],
                in1=gpo_t[:, c * 512:(c + 1) * 512], op=ALU.mult), reads=[B_gpo], writes=[B_gag])

    def mods_feat(which, grp):
        base = 12 * which + 4 * grp
        for c in range(4):
            mod_feature(("ada", base + c), grp, c)
        evac_mod(2 * which + grp, 3 * which + grp)
        if grp == 1:
            mod_scale(2 * which + 1, 16 * which)

    def mods_bcast(which):
        base = 12 * which
        csrep_t, bga_t, gpo_t = MT["csrep"], MT["bga"], MT["gpo"]
        P.op("dve", lambda e: e.tensor_copy(out=csrep_t.rearrange("p k j m -> p (k j) m"),
                                            in_=csb_t.unsqueeze(2).to_broadcast([128, 32, 128])),
             reads=[B_csb], writes=[B_csrep])
        P.dma("sp", bga_t, DI("b_ada")[0:1, (2 + 3 * which) * D:(3 + 3 * which) * D].to_broadcast([128, D]),
              sem_m, writes=[B_bga])
        P.dma("sp", gpo_t, DI("gpost")[which:which + 1, :].to_broadcast([128, D]), sem_m2, writes=[B_gpo])
        for c in range(4):
            mod_bcast(("ada", base + 8 + c), c)

    mods_feat(0, 0)
    mods_feat(0, 1)
    P.mark('mods0')

    def pass_layout(ntp):
        T = ntp * 128
        o = O_BIG
        L = {}

        def tk(name, shape, dtype):
            nonlocal o
            esz = 2 if dtype == BF16 else 4
            n = 1
            for s_ in shape[1:]:
                n *= s_
            L[name] = AR.view(o, shape, dtype)
            o += (n * esz + 31) // 32 * 32
        tk("hT", [128, 16, T], BF16)
        tk("oT", [128, 8, T], BF16)
        L["_tmp0"] = o
        tk("xbuf", [128, 3, D], F32)
        tk("xn", [128, 2, D], BF16)
        tk("tmpa", [128, 2, 8, 128], F32)
        tk("junkp", [128, D], BF16)
        o = L["_tmp0"]
        tk("lrT", [128, T], F32)
        tk("nla", [128, ntp, 2, 128], F32)
        L["zt"] = AR.view(o, [128, ntp, 2, 128], F32)
        tk("E", [128, 2, 3, 512], F32)
        tk("qin", [128, 2, T], BF16)
        tk("kin", [128, 2, T], BF16)
        tk("koT", [128, 2, 512], BF16)
        tk("ko", [128, 2, ntp, 128], BF16)
        tk("v", [128, 2, ntp, 256], BF16)
        tk("sgn", [128, 2, ntp, 256], BF16)
        tk("sgt", [128, 2, 256], F32)
        tk("gnb", [128, 2, 256], F32)
        tk("snapF", [128, 2 * ntp + 1, 256], BF16)
        tk("snapB", [128, 2 * ntp + 1, 256], BF16)
        tk("S", [128, 2, 2, 256], F32)
        tk("dec", [128, 2, 2 * ntp], F32)
        tk("attT", [128, 2, 2, 128], BF16)
        tk("og", [128, 2, 256], BF16)
        assert o <= AR.nbytes, (o, AR.nbytes)
        print("pass layout ntp", ntp, "end", o, "of", AR.nbytes)
        return L

    sem_x = [P.dsem("x0"), P.dsem("x1"), P.dsem("x2")]
    sem_o = P.dsem("out")
    sem_so = [[P.dsem("so00"), P.dsem("so01")], [P.dsem("so10"), P.dsem("so11")]]
    sem_si = [P.dsem("stin0"), P.dsem("stin1")]
    sem_g = [P.dsem("gn0"), P.dsem("gn1")]
    out_evs = []
    x1_evs = []
    fin = []
    sem_x1 = P.dsem("x1st")
    sem_xr = P.dsem("x3")
    sem_os = [P.dsem(f"os{i}") for i in range(6)]
    sem_f = [P.dsem(f"x1f{i}") for i in range(6)]
    B_st2 = Buf("st2")
    SC = 128 ** -0.5
    LNSC = float(np.log(SC))

    def run_pass(pidx):
        t0, ntp = PASSES[pidx]
        T = ntp * 128
        n5 = T // 512
        nch = 2 * ntp
        j = pidx
        L = pass_layout(ntp)
        if pidx > 0:
            P.barrier()
        hT, oT = L["hT"], L["oT"]
        B_hT = [Buf(f"hT{t}") for t in range(ntp)]
        B_oT = [Buf(f"oT{t}") for t in range(ntp)]
        xbuf, xn, tmpa, junkp = L["xbuf"], L["xn"], L["tmpa"], L["junkp"]
        B_xb = [Buf(), Buf(), Buf()]
        B_xn = [Buf(), Buf()]
        B_tmpa = [Buf(), Buf()]
        B_junk0 = Buf()
        B_ss = [Buf() for _ in range(ntp)]
        B_rs = [Buf() for _ in range(ntp)]

        def p2_front(tl):
            t = t0 + tl
            b = tl % 3
            P.dma("sp", xbuf[:, b, :], DI("xin")[t * 128:(t + 1) * 128, :], sem_x[b], writes=[B_xb[b]])
            P.op("act", lambda e: e.activation(out=junkp, in_=xbuf[:, b, :], func=AF.Square, accum_out=stat_t[:, t:t + 1]),
                 reads=[B_xb[b], B_stat], writes=[B_junk0, B_ss[tl]])
            P.op("dve", lambda e: e.tensor_scalar(out=stat_t[:, 16 + t:17 + t], in0=stat_t[:, t:t + 1],
                                                  scalar1=1.0 / D, scalar2=EPS, op0=ALU.mult, op1=ALU.add),
                 reads=[B_ss[tl]], writes=[B_rs[tl]])
            P.op("dve", lambda e: e.reciprocal(out=stat_t[:, 16 + t:17 + t], in_=stat_t[:, 16 + t:17 + t]),
                 writes=[B_rs[tl]])

        def p2_back(tl):
            t = t0 + tl
            b = tl % 3
            nb_ = tl % 2
            P.op("act", lambda e: e.activation(out=stat_t[:, 16 + t:17 + t], in_=stat_t[:, 16 + t:17 + t], func=AF.Sqrt),
                 writes=[B_rs[tl]])
            P.op("act", lambda e: e.activation(out=xn[:, nb_, :], in_=xbuf[:, b, :], func=AF.Copy, scale=stat_t[:, 16 + t:17 + t]),
                 reads=[B_xb[b], B_rs[tl]], writes=[B_xn[nb_]])
            for half in range(2):
                bank = 6 + half

                def fn(e, half=half, bank=bank):
                    ins = None
                    for kk in range(8):
                        kc = half * 8 + kk
                        ins = e.transpose(out=psv(bank, [128, 8, 128], BF16)[:, kk, :],
                                          in_=xn[:, nb_, kc * 128:(kc + 1) * 128], identity=ident_t)
                    return ins
                P.op("pe", fn, reads=[B_xn[nb_], B_const], writes=[PSB[bank]])
                h8 = half * 8
                P.op("dve", lambda e, half=half, bank=bank, h8=h8: e.tensor_tensor(
                    out=tmpa[:, half, :, :], in0=psv(bank, [128, 8, 128], BF16),
                    in1=modF_t[:, 1, h8:h8 + 8, j:j + 1].to_broadcast([128, 8, 128]), op=ALU.mult),
                    reads=[PSB[bank], B_modF], writes=[B_tmpa[half]])
                P.op("dve", lambda e, half=half, h8=h8: e.tensor_tensor(
                    out=hT[:, h8:h8 + 8, tl * 128:(tl + 1) * 128], in0=tmpa[:, half, :, :],
                    in1=modF_t[:, 0, h8:h8 + 8, j:j + 1].to_broadcast([128, 8, 128]), op=ALU.add),
                    reads=[B_tmpa[half], B_modF], wacc=[B_hT[tl]])
        for tl in range(ntp + 1):
            if tl < ntp:
                p2_front(tl)
            if tl >= 1:
                p2_back(tl - 1)
        P.mark(f'p{pidx}.P2')
        if stage < 2:
            return L, B_hT, B_oT

        lrT, nla, zt, E = L["lrT"], L["nla"], L["zt"], L["E"]
        qin, kin, koT, ko, v2, sgn2, sgt, gnb2 = L["qin"], L["kin"], L["koT"], L["ko"], L["v"], L["sgn"], L["sgt"], L["gnb"]
        snapF, snapB, S, dec, attT, og = L["snapF"], L["snapB"], L["S"], L["dec"], L["attT"], L["og"]
        B_lrT = Buf("lrT")
        slot, wb = ring.acquire(("lr", pidx))
        wv = ring.slot_view(slot, [128, 16, 32], BF16)
        P.barrier()
        P.op("dve", lambda e: e.memset(lrT[32:33, :], 1.0), wacc=[B_lrT])
        for q5 in range(n5):
            def fn(e, q5=q5):
                ins = None
                for k in range(16):
                    ins = e.matmul(psv(0, [128, 512])[0:32, :], lhsT=wv[:, k, :], rhs=hT[:, k, q5 * 512:(q5 + 1) * 512],
                                   start=(k == 0), stop=(k == 15))
                return ins
            P.op("pe", fn, reads=[wb] + B_hT[q5 * 4:(q5 + 1) * 4], writes=[PSB[0]])
            P.op("dve", lambda e, q5=q5: e.tensor_copy(out=lrT[0:32, q5 * 512:(q5 + 1) * 512], in_=psv(0, [128, 512])[0:32, :]),
                 reads=[PSB[0]], wacc=[B_lrT])
        ring.release(("lr", pidx))
        P.mark(f'p{pidx}.lr')

        B_nla, B_E = Buf(), Buf()
        B_zt = [B_E, B_E]
        B_qin, B_kin, B_koT, B_ko, B_v2, B_sgn2, B_sgt, B_gnb2 = Buf(), Buf(), Buf(), Buf(), [Buf(), Buf()], [Buf(), Buf()], [Buf(), Buf()], [Buf(), Buf()]
        B_snapF, B_snapBa, B_S, B_dec, B_attT, B_og = Buf(), Buf(), [[Buf(), Buf()], [Buf(), Buf()]], Buf(), [Buf(), Buf()], [Buf(), Buf()]
        a_fb = a_fb_t
        def prepc_gen(hh):
            hq = hh % 2
            vv, ss_, Bv, Bs = v2[:, hq, :, :], sgn2[:, hq, :, :], B_v2[hq], B_sgn2[hq]
            gq = gnb2[:, hq, :]
            slot, wb = ring.acquire(("vg", pidx, hh))
            wvg = ring.slot_view(slot, [128, 16, 512], BF16)
            P.dma("sp", gq, DI("gn")[0:1, hh * 256:(hh + 1) * 256].to_broadcast([128, 256]), sem_g[hq], writes=[B_gnb2[hq]])
            for tl in range(ntp):
                vb = 4 + tl % 2

                def fn(e, tl=tl, vb=vb):
                    ins = None
                    for k in range(16):
                        ins = e.matmul(psv(vb, [128, 512]), lhsT=hT[:, k, tl * 128:(tl + 1) * 128], rhs=wvg[:, k, :],
                                       start=(k == 0), stop=(k == 15))
                    return ins
                P.op("pe", fn, reads=[wb, B_hT[tl]], writes=[PSB[vb]])
                sb_ = tl % 2
                P.op("act", lambda e, tl=tl, vb=vb: e.activation(out=vv[:, tl, :], in_=psv(vb, [128, 512])[:, 0:256], func=AF.Copy),
                     reads=[PSB[vb]], wacc=[Bv])
                P.op("act", lambda e, sb_=sb_, vb=vb: e.activation(out=sgt[:, sb_, :], in_=psv(vb, [128, 512])[:, 256:512], func=AF.Silu),
                     reads=[PSB[vb]], writes=[B_sgt[sb_]])
                P.op("pool", lambda e, sb_=sb_, tl=tl: e.tensor_tensor(out=ss_[:, tl, :], in0=sgt[:, sb_, :], in1=gq, op=ALU.mult),
                     reads=[B_sgt[sb_], B_gnb2[hq]], wacc=[Bs])
                yield
            ring.release(("vg", pidx, hh))

        GS = int(os.environ.get("GLA_STOP", "99"))

        def prepa(h):
            for pr in range(ntp // 2):
                bk = pr % 4

                def fn(e, pr=pr, bk=bk):
                    ins = None
                    for t2 in range(2):
                        tl = 2 * pr + t2
                        for d_ in range(2):
                            ins = e.matmul(psv(bk, [128, 2, 2, 128])[:, t2, d_, :], lhsT=lrT[0:33, tl * 128:(tl + 1) * 128],
                                           rhs=a_fb[0:33, d_, h * 128:(h + 1) * 128], start=True, stop=True)
                    return ins
                P.op("pe", fn, reads=[B_lrT, B_const], writes=[PSB[bk]])
                P.op("act", lambda e, pr=pr, bk=bk: e.activation(out=zt[:, 2 * pr:2 * pr + 2, :, :], in_=psv(bk, [128, 2, 2, 128]),
                                                                func=AF.Exp, scale=-1.0), reads=[PSB[bk]], wacc=[B_zt[0]])
            P.op("act", lambda e: e.activation(out=nla, in_=zt, func=AF.Ln, bias=1.0), reads=[B_zt[0]], writes=[B_nla])

        def do_head(h):
            if GS <= 0:
                return
            if h == 0:
                prepa(0)
            P.mark(f'p{pidx}.h{h}.prepa')
            if GS <= 1:
                return
            slot, wb = ring.acquire(("qk", pidx, h))
            wqk = ring.slot_view(slot, [128, 16, 256], BF16)
            for q5 in range(n5):
                for t4 in range(4):
                    tl = q5 * 4 + t4
                    cb = (1, 0, 5, 6)[tl % 4]

                    def fn(e, tl=tl, cb=cb):
                        ins = None
                        for d_ in range(2):
                            ins = e.matmul(psv(cb, [128, 2, 256])[:, d_, :], lhsT=nla[:, tl, d_, :], rhs=cum_t[:, d_, :],
                                           start=True, stop=True)
                        return ins
                    P.op("pe", fn, reads=[B_nla, B_const], writes=[PSB[cb]])
                    cs_ = slice(t4 * 128, (t4 + 1) * 128)
                    pc = psv(cb, [128, 2, 2, 128])
                    P.op("act", lambda e, cs_=cs_, pc=pc: e.activation(out=E[:, :, 0:3:2, cs_], in_=pc, func=AF.Exp),
                         reads=[PSB[cb]], wacc=[B_E])
                    P.op("act", lambda e, cs_=cs_, pc=pc: e.activation(out=E[:, :, 1, cs_], in_=pc[:, :, 0, :], func=AF.Exp, scale=-1.0),
                         reads=[PSB[cb]], wacc=[B_E])
                    c0_ = t4 * 128
                    P.op("dve", lambda e, tl=tl, c0_=c0_: e.tensor_copy(out=dec[:, 0, 2 * tl:2 * tl + 2], in_=E[:, 0, 0, c0_ + 63:c0_ + 128:64]),
                         reads=[B_E], wacc=[B_dec])
                    P.op("dve", lambda e, tl=tl, c0_=c0_: e.tensor_copy(out=dec[:, 1, 2 * tl:2 * tl + 2], in_=E[:, 1, 0, c0_:c0_ + 128:64]),
                         reads=[B_E], wacc=[B_dec])
                for qk in range(2):
                    def fn(e, qk=qk, q5=q5):
                        ins = None
                        for k in range(16):
                            ins = e.matmul(psv(2 + qk, [128, 512]), lhsT=wqk[:, k, qk * 128:(qk + 1) * 128],
                                           rhs=hT[:, k, q5 * 512:(q5 + 1) * 512], start=(k == 0), stop=(k == 15))
                        return ins
                    P.op("pe", fn, reads=[wb] + B_hT[q5 * 4:(q5 + 1) * 4], writes=[PSB[2 + qk]])
                c5 = slice(q5 * 512, (q5 + 1) * 512)
                for d_ in range(2):
                    P.op("dve", lambda e, d_=d_, c5=c5: e.scalar_tensor_tensor(out=qin[:, d_, c5], in0=psv(2, [128, 512]), scalar=SC,
                                                                               in1=E[:, d_, 0, :], op0=ALU.mult, op1=ALU.mult),
                         reads=[PSB[2], B_E], wacc=[B_qin])
                    P.op("dve", lambda e, d_=d_, c5=c5: e.tensor_tensor(out=kin[:, d_, c5], in0=psv(3, [128, 512]), in1=E[:, d_, 1, :], op=ALU.mult),
                         reads=[PSB[3], B_E], wacc=[B_kin])
                    P.op("dve", lambda e, d_=d_: e.tensor_tensor(out=koT[:, d_, :], in0=psv(3, [128, 512]), in1=E[:, d_, 2, :], op=ALU.mult),
                         reads=[PSB[3], B_E], wacc=[B_koT])
                def fn(e):
                    ins = None
                    for d_ in range(2):
                        for t4 in range(4):
                            ins = e.transpose(out=psv(4, [128, 2, 4, 128], BF16)[:, d_, t4, :], in_=koT[:, d_, t4 * 128:(t4 + 1) * 128],
                                              identity=ident_t)
                    return ins
                P.op("pe", fn, reads=[B_koT, B_const], writes=[PSB[4]])
                P.op("dve", lambda e, q5=q5: e.tensor_copy(out=ko[:, :, q5 * 4:(q5 + 1) * 4, :], in_=psv(4, [128, 2, 4, 128], BF16)),
                     reads=[PSB[4]], wacc=[B_ko])
            ring.release(("qk", pidx, h))
            P.mark(f'p{pidx}.h{h}.prepb')
            if GS <= 2:
                return
            hp = h % 2
            v, sgn, B_v, B_sgn = v2[:, hp, :, :], sgn2[:, hp, :, :], B_v2[hp], B_sgn2[hp]
            if h == 0:
                for _ in prepc_gen(0):
                    pass
            P.mark(f'p{pidx}.h{h}.prepc')
            if GS <= 3:
                return
            seg0 = 0 if pidx == 0 else 4
            lk = link_t[:, pidx:pidx + 1]

            def chain(d_, order, s0_ap, snap_of, out_t, kbanks):
                cur = 0
                if s0_ap is not None:
                    P.dma("sp", S[:, d_, cur, :], s0_ap, sem_si[d_], writes=[B_S[d_][cur]])
                else:
                    P.op("dve", lambda e, cur=cur: e.memset(S[:, d_, cur, :], 0.0), writes=[B_S[d_][cur]])
                sa, sbuf = snap_of(order[0])
                P.op("act", lambda e, sa=sa, cur=cur: e.activation(out=sa, in_=S[:, d_, cur, :], func=AF.Copy), reads=[B_S[d_][cur]], wacc=[sbuf])
                yield
                for ci, c in enumerate(order):
                    rows = slice((c % 2) * 64, (c % 2) * 64 + 64)
                    tl = c // 2
                    kb = kbanks[ci % 2]
                    P.op("pe", lambda e, rows=rows, tl=tl, kb=kb: e.matmul(psv(kb, [128, 2, 256])[:, d_, :], lhsT=ko[rows, d_, tl, :],
                                                                      rhs=v[rows, tl, :], start=True, stop=True),
                         reads=[B_ko, B_v], writes=[PSB[kb]])
                    nxt = 1 - cur
                    P.op("dve", lambda e, c=c, cur=cur, nxt=nxt, kb=kb: e.scalar_tensor_tensor(
                        out=S[:, d_, nxt, :], in0=S[:, d_, cur, :], scalar=dec[:, d_, c:c + 1], in1=psv(kb, [128, 2, 256])[:, d_, :],
                        op0=ALU.mult, op1=ALU.add), reads=[B_S[d_][cur], B_dec, PSB[kb]], writes=[B_S[d_][nxt]])
                    cur = nxt
                    last = (ci == len(order) - 1)
                    seg_end = last or ((c % 4 == 3) if d_ == 0 else (c % 4 == 0))
                    if seg_end:
                        seg = seg0 + c // 4
                        out_evs.append(P.dma("sp", out_t[seg, h], S[:, d_, cur, :], sem_so[d_][cur], reads=[B_S[d_][cur]]))
                    if not last:
                        cn = order[ci + 1]
                        sa, sbuf = snap_of(cn)
                        if seg_end:
                            nxt = 1 - cur
                            P.op("dve", lambda e, cur=cur, nxt=nxt: e.tensor_scalar_mul(out=S[:, d_, nxt, :], in0=S[:, d_, cur, :], scalar1=lk),
                                 reads=[B_S[d_][cur], B_const], writes=[B_S[d_][nxt]])
                            cur = nxt
                        P.op("act", lambda e, sa=sa, cur=cur: e.activation(out=sa, in_=S[:, d_, cur, :], func=AF.Copy), reads=[B_S[d_][cur]], wacc=[sbuf])
                    yield

            gf = chain(0, list(range(nch)), DI("s0f")[h] if pidx == 0 else None,
                       lambda c: (snapF[:, c, :], B_snapF), snf, (0, 3))
            gb = chain(1, list(range(nch - 1, -1, -1)), DI("s0b")[h] if pidx == 0 else None,
                       lambda c: (snapB[:, c, :], B_snapBa), snb, (1, 2))
            pg = prepc_gen(h + 1) if (h + 1 < 4 and GS >= 99) else iter(())
            for i_ in range(nch + 2):
                next(gf, None)
                next(gb, None)
                if i_ % 2 == 1:
                    next(pg, None)
            for _ in pg:
                pass
            if h + 1 < 4 and GS >= 99:
                prepa(h + 1)
            P.mark(f'p{pidx}.h{h}.fwd')
            if GS <= 4:
                return
            o_bank = lambda tl: 4 + tl // 2
            B_sso = Buf()

            def o_tile(tl):
                ab = tl % 2
                cs_ = slice(tl * 128, (tl + 1) * 128)

                atb = (1, 0)[tl % 2]

                def fn(e):
                    ins = None
                    for d_ in range(2):
                        ins = e.matmul(psv(atb, [128, 2, 128])[:, d_, :], lhsT=kin[:, d_, cs_], rhs=qin[:, d_, cs_], start=True, stop=True)
                    return ins
                P.op("pe", fn, reads=[B_kin, B_qin], writes=[PSB[atb]])
                P.op("dve", lambda e: e.tensor_tensor(out=attT[:, ab, :, :], in0=psv(atb, [128, 2, 128]), in1=amask_t, op=ALU.mult),
                     reads=[PSB[atb], B_const], writes=[B_attT[ab]])
                po = psv(o_bank(tl), [128, 2, 256])[:, tl % 2, :]

                def fn2(e):
                    e.matmul(po, lhsT=attT[:, ab, 0, :], rhs=v[:, tl, :], start=True, stop=False)
                    e.matmul(po, lhsT=attT[:, ab, 1, :], rhs=v[:, tl, :], start=False, stop=False)
                    ins = None
                    for cc in range(2):
                        c = 2 * tl + cc
                        rows = slice(cc * 64, cc * 64 + 64)
                        ccs = slice(c * 64, (c + 1) * 64)
                        e.matmul(po[rows, :], lhsT=qin[:, 0, ccs], rhs=snapF[:, c, :], start=False, stop=False)
                        ins = e.matmul(po[rows, :], lhsT=qin[:, 1, ccs], rhs=snapB[:, c, :], start=False, stop=True)
                    return ins
                P.op("pe", fn2, reads=[B_attT[ab], B_v, B_qin, B_snapF, B_snapBa], wacc=[PSB[o_bank(tl)]])
                col = 32 + h * 12 + t0 + tl
                P.op("act", lambda e: e.activation(out=og[:, 0, :], in_=po, func=AF.Square, accum_out=stat_t[:, col:col + 1]),
                     reads=[PSB[o_bank(tl)]], writes=[B_og[0]], wacc=[B_sso])
            for tl in range(ntp):
                o_tile(tl)
            P.mark(f'p{pidx}.h{h}.bwd')
            if GS <= 5:
                return
            c0_ = 32 + h * 12 + t0
            c1_ = 96 + h * 12 + t0
            P.op("dve", lambda e: e.tensor_scalar(out=stat_t[:, c1_:c1_ + ntp], in0=stat_t[:, c0_:c0_ + ntp], scalar1=1.0 / 256,
                                                  scalar2=EPS, op0=ALU.mult, op1=ALU.add), reads=[B_sso], writes=[B_sso])
            P.op("dve", lambda e: e.reciprocal(out=stat_t[:, c1_:c1_ + ntp], in_=stat_t[:, c1_:c1_ + ntp]), reads=[], writes=[B_sso])
            P.op("act", lambda e: e.activation(out=stat_t[:, c1_:c1_ + ntp], in_=stat_t[:, c1_:c1_ + ntp], func=AF.Sqrt),
                 reads=[], writes=[B_sso])
            for tl in range(ntp):
                ob = tl % 2
                po = psv(o_bank(tl), [128, 2, 256])[:, tl % 2, :]
                P.op("dve", lambda e, tl=tl, ob=ob, po=po: e.scalar_tensor_tensor(
                    out=og[:, ob, :], in0=po, scalar=stat_t[:, c1_ + tl:c1_ + tl + 1], in1=sgn[:, tl, :], op0=ALU.mult, op1=ALU.mult),
                    reads=[PSB[o_bank(tl)], B_sso, B_sgn], writes=[B_og[ob]])

                eb = 2 + tl % 2

                def fn(e, ob=ob, eb=eb):
                    ins = None
                    for vc in range(2):
                        ins = e.transpose(out=psv(eb, [128, 2, 128], BF16)[:, vc, :], in_=og[:, ob, vc * 128:(vc + 1) * 128], identity=ident_t)
                    return ins
                P.op("pe", fn, reads=[B_og[ob], B_const], writes=[PSB[eb]])
                P.op("act", lambda e, tl=tl, eb=eb: e.activation(out=oT[:, 2 * h:2 * h + 2, tl * 128:(tl + 1) * 128],
                                                         in_=psv(eb, [128, 2, 128], BF16), func=AF.Copy),
                     reads=[PSB[eb]], wacc=[B_oT[tl]])

        for h in range(4 if GS >= 99 else 1):
            do_head(h)
            if stage >= 4 and pidx == 1 and h < 2 and GS >= 99:
                mods_feat(1, h)
        if GS < 99:
            return L, B_hT, B_oT

        P.mark(f'p{pidx}.gla_done')
        if stage < 3:
            return L, B_hT, B_oT
        P.barrier()
        tb = L["_tmp0"]
        if pidx == 0:
            set_mods_tmp(tb)
            mods_bcast(0)
            set_mods_tmp(O_BIG)
            P.barrier()
        fT = AR.view(tb, [128, 8, T], BF16)
        foT = AR.view(tb + 16 * T, [128, 2, 2, T], BF16)
        Yt = AR.view(tb + 16 * T + 8 * T, [128, 2, ntp, 512], BF16)
        B_fT = [Buf() for _ in range(n5)]
        B_foT, B_Y = [Buf(), Buf()], [Buf(), Buf()]
        rot = {"a": 0, "b": 0, "c": 0}

        def nb(grp, banks):
            b_ = banks[rot[grp] % len(banks)]
            rot[grp] += 1
            return b_

        def do_group(g):
            gb = g % 2
            slot, wb = ring.acquire(("fo", pidx, g))
            wfo = ring.slot_view(slot, [128, 16, 256], BF16)
            for cc in range(2):
                for q5 in range(n5):
                    bk = nb("a", (0, 1, 2, 3))

                    def fn(e, cc=cc, q5=q5, bk=bk):
                        ins = None
                        for k in range(16):
                            ins = e.matmul(psv(bk, [128, 512]), lhsT=wfo[:, k, cc * 128:(cc + 1) * 128],
                                           rhs=hT[:, k, q5 * 512:(q5 + 1) * 512], start=(k == 0), stop=(k == 15))
                        return ins
                    P.op("pe", fn, reads=[wb] + B_hT[q5 * 4:(q5 + 1) * 4], writes=[PSB[bk]])
                    dst = foT[:, gb, cc, q5 * 512:(q5 + 1) * 512]
                    if (cc + q5) % 2 == 0:
                        P.op("act", lambda e, dst=dst, bk=bk: e.activation(out=dst, in_=psv(bk, [128, 512]), func=AF.Copy),
                             reads=[PSB[bk]], wacc=[B_foT[gb]])
                    else:
                        P.op("dve", lambda e, dst=dst, bk=bk: e.tensor_copy(out=dst, in_=psv(bk, [128, 512])),
                             reads=[PSB[bk]], wacc=[B_foT[gb]])
            ring.release(("fo", pidx, g))
            for tl in range(ntp):
                bk = nb("b", (4, 5))

                def fn(e, tl=tl, bk=bk):
                    ins = None
                    for cc in range(2):
                        ins = e.matmul(psv(bk, [128, 512]), lhsT=foT[:, gb, cc, tl * 128:(tl + 1) * 128], rhs=cs_t[:, cc, :],
                                       start=(cc == 0), stop=(cc == 1))
                    return ins
                P.op("pe", fn, reads=[B_foT[gb], B_const], writes=[PSB[bk]])
                dst = Yt[:, gb, tl, :]
                if tl % 2 == 0:
                    P.op("dve", lambda e, dst=dst, bk=bk: e.tensor_copy(out=dst, in_=psv(bk, [128, 512])),
                         reads=[PSB[bk]], wacc=[B_Y[gb]])
                else:
                    P.op("act", lambda e, dst=dst, bk=bk: e.activation(out=dst, in_=psv(bk, [128, 512]), func=AF.Copy),
                         reads=[PSB[bk]], wacc=[B_Y[gb]])
            for q5 in range(n5):
                slot, tbuf = ring.acquire(("tab", pidx, g, q5))
                tab = ring.slot_view(slot, [128, 2, ntp, 512], BF16)
                for cc in range(2):
                    bk = nb("c", (6, 7))

                    def fn(e, cc=cc, bk=bk, tab=tab):
                        ins = None
                        n = 0
                        for kt in range(ntp):
                            for sc_ in range(2):
                                ins = e.matmul(psv(bk, [128, 512]),
                                               lhsT=Yt[:, gb, kt, sc_ * 256 + cc * 128: sc_ * 256 + (cc + 1) * 128],
                                               rhs=tab[:, sc_, kt, :], start=(n == 0), stop=(n == 2 * ntp - 1))
                                n += 1
                        return ins
                    P.op("pe", fn, reads=[B_Y[gb], tbuf], writes=[PSB[bk]])
                    dst = fT[:, 2 * g + cc, q5 * 512:(q5 + 1) * 512]
                    if cc == 0:
                        P.op("act", lambda e, dst=dst, bk=bk: e.activation(out=dst, in_=psv(bk, [128, 512]), func=AF.Copy),
                             reads=[PSB[bk]], wacc=[B_fT[q5]])
                    else:
                        P.op("dve", lambda e, dst=dst, bk=bk: e.tensor_copy(out=dst, in_=psv(bk, [128, 512])),
                             reads=[PSB[bk]], wacc=[B_fT[q5]])
                ring.release(("tab", pidx, g, q5))
        for g in range(4):
            do_group(g)

        P.mark(f'p{pidx}.fnet')
        if pidx == 1:
            dump_fm("oTB", oT, 8, T, B_oT)
            dump_fm("fTB", fT, 8, T, B_fT)
        P.barrier()
        mo = tb + 16 * T
        merged = AR.view(mo, [128, 16, T], BF16)
        gsb = AR.view(mo + 32 * T, [128, 2, 2, 512], F32)
        t12 = AR.view(mo + 32 * T + 8192, [128, 2, 2, 512], F32)
        B_mg = [Buf() for _ in range(ntp)]
        B_gs, B_t12 = [[Buf(), Buf()], [Buf(), Buf()]], [[Buf(), Buf()], [Buf(), Buf()]]
        step = [0]

        def do_jp(jp):
            slot, wgb = ring.acquire(("gate", pidx, jp))
            wg = ring.slot_view(slot, [128, 2, 16, 256], BF16)
            slot2, wbb = ring.acquire2(("br", pidx, jp))
            wbr = ring.slot_view(slot2, [128, 2, 8, 256], BF16)
            def do_step(jj, q5):
                jch = 2 * jp + jj
                db = step[0] % 2
                step[0] += 1
                c5 = slice(q5 * 512, (q5 + 1) * 512)
                for ab in range(2):
                    bk = 4 * db + ab

                    def fn(e, ab=ab, bk=bk):
                        ins = None
                        for k in range(16):
                            ins = e.matmul(psv(bk, [128, 512]), lhsT=wg[:, ab, k, jj * 128:(jj + 1) * 128], rhs=hT[:, k, c5],
                                           start=(k == 0), stop=(k == 15))
                        return ins
                    P.op("pe", fn, reads=[wgb] + B_hT[q5 * 4:(q5 + 1) * 4], writes=[PSB[bk]])
                    P.op("act", lambda e, ab=ab, bk=bk: e.activation(out=gsb[:, db, ab, :], in_=psv(bk, [128, 512]), func=AF.Sigmoid,
                                                                   bias=bgatel_t[:, ab * 16 + jch: ab * 16 + jch + 1]),
                         reads=[PSB[bk], B_const], writes=[B_gs[db][ab]])
                if jj == 1 and q5 == n5 - 1:
                    ring.release(("gate", pidx, jp))
                for ab in range(2):
                    bk = 4 * db + 2 + ab
                    src_ = oT if ab == 0 else fT
                    srcb = (B_oT if ab == 0 else B_fT)

                    def fn(e, ab=ab, bk=bk, src_=src_):
                        ins = None
                        for k in range(8):
                            ins = e.matmul(psv(bk, [128, 512]), lhsT=wbr[:, ab, k, jj * 128:(jj + 1) * 128], rhs=src_[:, k, c5],
                                           start=(k == 0), stop=(k == 7))
                        return ins
                    rb = B_oT[q5 * 4:(q5 + 1) * 4] if ab == 0 else [B_fT[q5]]
                    P.op("pe", fn, reads=[wbb] + rb, writes=[PSB[bk]])
                    P.op("dve", lambda e, ab=ab, bk=bk: e.tensor_tensor(out=t12[:, db, ab, :], in0=psv(bk, [128, 512]), in1=gsb[:, db, ab, :],
                                                                      op=ALU.mult), reads=[PSB[bk], B_gs[db][ab]], writes=[B_t12[db][ab]])
                P.op("dve", lambda e: e.tensor_tensor(out=merged[:, jch, c5], in0=t12[:, db, 0, :], in1=t12[:, db, 1, :], op=ALU.add),
                     reads=[B_t12[db][0], B_t12[db][1]], wacc=B_mg[q5 * 4:(q5 + 1) * 4])

            for jj in range(2):
                for q5 in range(n5):
                    do_step(jj, q5)
            ring.release(("br", pidx, jp))
        for jp in range(8):
            do_jp(jp)

        P.mark(f'p{pidx}.merge')
        if pidx == 1:
            dump_fm("mgB", merged, 16, T, B_mg)
        P.barrier()
        mt = AR.view(O_BIG, [128, ntp, D], F32)
        xb2 = AR.view(mo + 32 * T, [128, 2, D], F32)
        B_m = [Buf() for _ in range(ntp)]
        B_xb2 = [Buf(), Buf()]
        B_st1 = Buf()
        junk = AR.view(mo + 32 * T + 16 * KB, [128, D], BF16)
        B_junk = Buf()
        B_pp = [Buf() for _ in range(ntp)]
        for c in range(4):
            slot, wob = ring.acquire(("wo", pidx, c))
            wo = ring.slot_view(slot, [128, 16, 512], BF16)
            for tl in range(ntp):
                bk = (c * ntp + tl) % 8

                def fn(e, tl=tl, bk=bk, wo=wo):
                    ins = None
                    for k in range(16):
                        ins = e.matmul(psv(bk, [128, 512]), lhsT=merged[:, k, tl * 128:(tl + 1) * 128], rhs=wo[:, k, :],
                                       start=(k == 0), stop=(k == 15))
                    return ins
                P.op("pe", fn, reads=[wob, B_mg[tl]], writes=[PSB[bk]])
                dst = mt[:, tl, c * 512:(c + 1) * 512]
                pcol = (t0 + tl) * 4 + c
                P.op("act", lambda e, bk=bk, pcol=pcol: e.activation(out=junk[:, 0:512], in_=psv(bk, [128, 512]), func=AF.Square,
                                                                   accum_out=part_t[:, pcol:pcol + 1]),
                     reads=[PSB[bk], B_part], writes=[B_junk], wacc=[B_pp[tl]])
                P.op("dve", lambda e, dst=dst, bk=bk, c=c: e.tensor_tensor(out=dst, in0=psv(bk, [128, 512]),
                                                                         in1=gag_t[:, j, c * 512:(c + 1) * 512], op=ALU.mult),
                     reads=[PSB[bk], B_gag], wacc=[B_m[tl]])
            ring.release(("wo", pidx, c))
        P.mark(f'p{pidx}.wout')
        pv = part_t[:, t0 * 4:(t0 + ntp) * 4].rearrange("p (t c) -> p t c", c=4)
        P.op("dve", lambda e: e.tensor_tensor(out=stat_t[:, 144 + t0:144 + t0 + ntp], in0=pv[:, :, 0], in1=pv[:, :, 1], op=ALU.add),
             reads=B_pp, writes=[B_st1])
        P.op("dve", lambda e: e.tensor_tensor(out=stat_t[:, 144 + t0:144 + t0 + ntp], in0=stat_t[:, 144 + t0:144 + t0 + ntp], in1=pv[:, :, 2], op=ALU.add),
             writes=[B_st1])
        P.op("dve", lambda e: e.tensor_tensor(out=stat_t[:, 144 + t0:144 + t0 + ntp], in0=stat_t[:, 144 + t0:144 + t0 + ntp], in1=pv[:, :, 3], op=ALU.add),
             writes=[B_st1])
        c0_, c1_ = 144 + t0, 160 + t0
        P.op("dve", lambda e: e.tensor_scalar(out=stat_t[:, c1_:c1_ + ntp], in0=stat_t[:, c0_:c0_ + ntp], scalar1=1.0 / D, scalar2=EPS,
                                              op0=ALU.mult, op1=ALU.add), reads=[B_st1], writes=[B_st1])
        P.op("dve", lambda e: e.reciprocal(out=stat_t[:, c1_:c1_ + ntp], in_=stat_t[:, c1_:c1_ + ntp]), writes=[B_st1])
        P.op("act", lambda e: e.activation(out=stat_t[:, c1_:c1_ + ntp], in_=stat_t[:, c1_:c1_ + ntp], func=AF.Sqrt), writes=[B_st1])
        xb4 = [xb2[:, 0, :], xb2[:, 1, :], AR.view(mo, [128, D], F32), AR.view(mo + 8 * KB, [128, D], F32)]
        B_xb4 = [B_xb2[0], B_xb2[1], Buf(), Buf()]
        sem_x4 = sem_x + [sem_xr]
        NBX = 4

        def xload(tn):
            bx = tn % NBX
            P.dma("sp", xb4[bx], DI("xin")[(t0 + tn) * 128:(t0 + tn + 1) * 128, :], sem_x4[bx], writes=[B_xb4[bx]],
                  extra=(B_st1.w if bx >= 2 else ()))
        for tn in range(min(NBX, ntp)):
            xload(tn)
        for it in range(ntp + 1):
            if it >= 1:
                tl = it - 1
                t = t0 + tl
                bx = tl % NBX
                P.op("dve", lambda e, tl=tl, bx=bx: e.scalar_tensor_tensor(out=mt[:, tl, :], in0=mt[:, tl, :], scalar=stat_t[:, c1_ + tl:c1_ + tl + 1],
                                                                         in1=xb4[bx], op0=ALU.mult, op1=ALU.add),
                     reads=[B_xb4[bx], B_st1], writes=[B_m[tl]])
                if tl + NBX < ntp:
                    xload(tl + NBX)
                P.op("act", lambda e, tl=tl, t=t: e.activation(out=junk, in_=mt[:, tl, :], func=AF.Square, accum_out=stat_t[:, 176 + t:177 + t]),
                     reads=[B_m[tl]], writes=[B_junk], wacc=[B_st2])
                x1_evs.append(P.dma("sp", x1s[t * 128:(t + 1) * 128, :], mt[:, tl, :], sem_x1, reads=[B_m[tl]]))
        P.mark(f'p{pidx}.resid')
        if "x1" in dbg and pidx == 0:
            for tl in range(ntp):
                fin.append(P.dma("sp", dbg["x1"][tl * 128:(tl + 1) * 128, :], mt[:, tl, :], sem_o, reads=[B_m[tl]]))
        return L, B_hT, B_oT

    B_t = Buf("dbgtmp")

    def dump_fm(name, ap, nchunk, T_, bufs):
        if name not in dbg:
            return
        tmpf = AR.view(AR.nbytes - 4 * KB, [128, 1024], F32)
        for k in range(nchunk):
            P.op("dve", lambda e, k=k: e.tensor_copy(out=tmpf[:, 0:T_], in_=ap[:, k, :]), reads=bufs, writes=[B_t])
            fin.append(P.dma("sp", dbg[name][:, k, :], tmpf[:, 0:T_], sem_o, reads=[B_t]))
    for pidx in range(2 if int(os.environ.get("GLA_STOP", "99")) >= 99 else 1):
        L, B_hT, B_oT = run_pass(pidx)
        if pidx == 0:
            if "hT" in dbg:
                tmpf = AR.view(AR.nbytes - 4 * KB, [128, 1024], F32)
                for k in range(16):
                    P.op("dve", lambda e, k=k, L=L, tmpf=tmpf: e.tensor_copy(out=tmpf, in_=L["hT"][:, k, :]), reads=B_hT, writes=[B_t])
                    fin.append(P.dma("sp", dbg["hT"][:, k, :], tmpf, sem_o, reads=[B_t]))
            if "oT" in dbg:
                tmpf = AR.view(AR.nbytes - 4 * KB, [128, 1024], F32)
                for k in range(8):
                    P.op("dve", lambda e, k=k, L=L, tmpf=tmpf: e.tensor_copy(out=tmpf, in_=L["oT"][:, k, :]), reads=B_oT, writes=[B_t])
                    fin.append(P.dma("sp", dbg["oT"][:, k, :], tmpf, sem_o, reads=[B_t]))
        if stage < 3:
            pass


    if stage >= 4:
        P.op("dve", lambda e: e.tensor_scalar(out=stat_t[:, 192:204], in0=stat_t[:, 176:188], scalar1=1.0 / D, scalar2=EPS,
                                              op0=ALU.mult, op1=ALU.add), reads=[B_st2], writes=[B_st2])
        P.op("dve", lambda e: e.reciprocal(out=stat_t[:, 192:204], in_=stat_t[:, 192:204]), writes=[B_st2])
        P.op("act", lambda e: e.activation(out=stat_t[:, 192:204], in_=stat_t[:, 192:204], func=AF.Sqrt), writes=[B_st2])
        P.barrier()
        mods_bcast(1)
        P.mark('mods1')
        TB = 768
        aT = AR.view(O_BIG, [128, 44, TB], BF16)
        R0 = O_BIG + 44 * TB * 2
        h2T = AR.view(R0, [128, 16, TB], BF16)
        x1p = AR.view(O_BIG + 48 * KB, [128, 2, D], F32)
        B_x1p_g = [Buf(), Buf()]
        xn2 = AR.view(R0 + 49152, [128, 2, D], BF16)
        sgt2 = AR.view(R0 + 49152, [128, 2, TB], F32)
        tmpb = AR.view(R0 + 57344, [128, 2, 8, 128], F32)
        BLK = {}
        yb = AR.view(R0, [128, 6, D], F32)
        x1f = AR.view(R0 + 49152, [128, 2, D], F32)
        junk2 = AR.view(O_BIG, [128, D], BF16)
        sem_p = [P.dsem("x1p0"), P.dsem("x1p1")]

        def ffn_block(fb):
            B_h2 = [Buf() for _ in range(6)]
            B_x1p, B_xn2 = B_x1p_g, [Buf(), Buf()]
            B_aT = Buf()
            B_sg = [Buf(), Buf()]
            B_y = [Buf() for _ in range(6)]
            B_x1f = [Buf(), Buf()]
            B_st3, B_j2 = Buf(), Buf()
            B_p3 = [Buf() for _ in range(6)]
            junk3 = AR.view(R0 + 49152, [128, 512], BF16)
            B_tmpb = [Buf(), Buf()]
            for tl in range(6):
                t = fb * 6 + tl
                b = tl % 2
                jc = 0 if t < 8 else 1
                if not (fb == 1 and tl < 2):
                    P.dma("sp", x1p[:, b, :], x1s[t * 128:(t + 1) * 128, :], sem_p[b], writes=[B_x1p[b]])
                P.op("act", lambda e, b=b, t=t: e.activation(out=xn2[:, b, :], in_=x1p[:, b, :], func=AF.Copy,
                                                           scale=stat_t[:, 192 + t:193 + t]), reads=[B_x1p[b], B_st2], writes=[B_xn2[b]])
                for half in range(2):
                    bank = 6 + half

                    def fn(e, half=half, b=b, bank=bank):
                        ins = None
                        for kk in range(8):
                            kc = half * 8 + kk
                            ins = e.transpose(out=psv(bank, [128, 8, 128], BF16)[:, kk, :],
                                              in_=xn2[:, b, kc * 128:(kc + 1) * 128], identity=ident_t)
                        return ins
                    P.op("pe", fn, reads=[B_xn2[b], B_const], writes=[PSB[bank]])
                    h8 = half * 8
                    P.op("dve", lambda e, half=half, bank=bank, h8=h8, jc=jc: e.tensor_tensor(
                        out=tmpb[:, half, :, :], in0=psv(bank, [128, 8, 128], BF16),
                        in1=modF_t[:, 3, h8:h8 + 8, jc:jc + 1].to_broadcast([128, 8, 128]), op=ALU.mult),
                        reads=[PSB[bank], B_modF], writes=[B_tmpb[half]])
                    P.op("dve", lambda e, half=half, h8=h8, jc=jc, tl=tl: e.tensor_tensor(
                        out=h2T[:, h8:h8 + 8, tl * 128:(tl + 1) * 128], in0=tmpb[:, half, :, :],
                        in1=modF_t[:, 2, h8:h8 + 8, jc:jc + 1].to_broadcast([128, 8, 128]), op=ALU.add),
                        reads=[B_tmpb[half], B_modF], wacc=[B_h2[tl]], extra=(BLK.get("st0", [])[0:3] if fb == 1 else ()))
            P.mark(f'f{fb}.prep')
            P.barrier()
            def do_hp(hp):
                slot, wb = ring.acquire(("f1", fb, hp))
                wgu = ring.slot_view(slot, [128, 2, 16, 256], BF16)
                for hh in range(2):
                    hc = 2 * hp + hh
                    bs = 4 * (hc % 2)

                    def fn(e, hh=hh, bs=bs):
                        ins = None
                        for gu in range(2):
                            for k in range(16):
                                lw = wgu[:, gu, k, hh * 128:(hh + 1) * 128]
                                e.matmul(psv(bs + 2 * gu, [128, 512]), lhsT=lw, rhs=h2T[:, k, 0:512], start=(k == 0), stop=(k == 15))
                                ins = e.matmul(psv(bs + 2 * gu + 1, [128, 512])[:, 0:256], lhsT=lw, rhs=h2T[:, k, 512:768],
                                               start=(k == 0), stop=(k == 15))
                        return ins
                    P.op("pe", fn, reads=[wb] + B_h2, writes=[PSB[bs], PSB[bs + 1], PSB[bs + 2], PSB[bs + 3]])
                    sb_ = hc % 2
                    P.op("act", lambda e, bs=bs, sb_=sb_: e.activation(out=sgt2[:, sb_, 0:512], in_=psv(bs, [128, 512]), func=AF.Silu),
                         reads=[PSB[bs]], writes=[B_sg[sb_]])
                    P.op("act", lambda e, bs=bs, sb_=sb_: e.activation(out=sgt2[:, sb_, 512:768], in_=psv(bs + 1, [128, 512])[:, 0:256], func=AF.Silu),
                         reads=[PSB[bs + 1]], wacc=[B_sg[sb_]])
                    P.op("dve", lambda e, bs=bs, sb_=sb_, hc=hc: e.tensor_tensor(out=aT[:, hc, 0:512], in0=psv(bs + 2, [128, 512]),
                                                                               in1=sgt2[:, sb_, 0:512], op=ALU.mult),
                         reads=[PSB[bs + 2], B_sg[sb_]], wacc=[B_aT])
                    P.op("dve", lambda e, bs=bs, sb_=sb_, hc=hc: e.tensor_tensor(out=aT[:, hc, 512:768], in0=psv(bs + 3, [128, 512])[:, 0:256],
                                                                               in1=sgt2[:, sb_, 512:768], op=ALU.mult),
                         reads=[PSB[bs + 3], B_sg[sb_]], wacc=[B_aT])
                ring.release(("f1", fb, hp))
            for hp in range(22):
                do_hp(hp)
            P.mark(f'f{fb}.F1')
            for c in range(4):
                for pz in range(4):
                    slot, wb = ring.acquire(("f2", fb, c, pz))
                    wd = ring.slot_view(slot, [128, 11, 512], BF16)
                    for tl in range(6):
                        def fn(e, tl=tl, pz=pz, wd=wd):
                            ins = None
                            for kk in range(11):
                                ins = e.matmul(psv(tl, [128, 512]), lhsT=aT[:, pz * 11 + kk, tl * 128:(tl + 1) * 128], rhs=wd[:, kk, :],
                                               start=(pz == 0 and kk == 0), stop=(pz == 3 and kk == 10))
                            return ins
                        if pz == 0:
                            P.op("pe", fn, reads=[wb, B_aT], writes=[PSB[tl]])
                        else:
                            P.op("pe", fn, reads=[wb, B_aT], wacc=[PSB[tl]])
                    ring.release(("f2", fb, c, pz))
                for tl in range(6):
                    dst = yb[:, tl, c * 512:(c + 1) * 512]
                    t = fb * 6 + tl
                    jc = 0 if t < 8 else 1
                    pcol = 48 + t * 4 + c
                    P.op("act", lambda e, tl=tl, pcol=pcol: e.activation(out=junk3, in_=psv(tl, [128, 512]), func=AF.Square,
                                                                       accum_out=part_t[:, pcol:pcol + 1]),
                         reads=[PSB[tl], B_part], writes=[B_j2], wacc=[B_p3[tl]])
                    P.op("dve", lambda e, dst=dst, tl=tl, jc=jc, c=c: e.tensor_tensor(out=dst, in0=psv(tl, [128, 512]),
                                                                                   in1=gag_t[:, jc, c * 512:(c + 1) * 512], op=ALU.mult),
                         reads=[PSB[tl], B_gag], wacc=[B_y[tl]])
            P.mark(f'f{fb}.F2')
            x1f6 = AR.view(O_BIG, [128, 6, D], F32)
            B_x1f6 = [Buf() for _ in range(6)]
            aT_dead = list(B_p3[5].w) + list(B_y[5].w)
            for tl in range(6):
                P.dma("sp", x1f6[:, tl, :], x1s[(fb * 6 + tl) * 128:(fb * 6 + tl + 1) * 128, :], sem_f[tl], writes=[B_x1f6[tl]],
                      extra=aT_dead)
            pv3 = part_t[:, 48 + fb * 24:48 + fb * 24 + 24].rearrange("p (t c) -> p t c", c=4)
            s3 = stat_t[:, 208 + fb * 6:208 + fb * 6 + 6]
            P.op("dve", lambda e: e.tensor_tensor(out=s3, in0=pv3[:, :, 0], in1=pv3[:, :, 1], op=ALU.add), reads=B_p3, writes=[B_st3])
            P.op("dve", lambda e: e.tensor_tensor(out=s3, in0=s3, in1=pv3[:, :, 2], op=ALU.add), writes=[B_st3])
            P.op("dve", lambda e: e.tensor_tensor(out=s3, in0=s3, in1=pv3[:, :, 3], op=ALU.add), writes=[B_st3])
            c0_, c1_ = 208 + fb * 6, 224 + fb * 6
            P.op("dve", lambda e: e.tensor_scalar(out=stat_t[:, c1_:c1_ + 6], in0=stat_t[:, c0_:c0_ + 6], scalar1=1.0 / D, scalar2=EPS,
                                                  op0=ALU.mult, op1=ALU.add), reads=[B_st3], writes=[B_st3])
            P.op("dve", lambda e: e.reciprocal(out=stat_t[:, c1_:c1_ + 6], in_=stat_t[:, c1_:c1_ + 6]), writes=[B_st3])
            P.op("act", lambda e: e.activation(out=stat_t[:, c1_:c1_ + 6], in_=stat_t[:, c1_:c1_ + 6], func=AF.Sqrt), writes=[B_st3])
            for it in range(7):
                if it >= 1:
                    tl = it - 1
                    t = fb * 6 + tl
                    P.op("dve", lambda e, tl=tl: e.scalar_tensor_tensor(out=yb[:, tl, :], in0=yb[:, tl, :], scalar=stat_t[:, c1_ + tl:c1_ + tl + 1],
                                                                       in1=x1f6[:, tl, :], op0=ALU.mult, op1=ALU.add),
                         reads=[B_x1f6[tl], B_st3], writes=[B_y[tl]])
                    ev_st = P.dma("sp", yout[t * 128:(t + 1) * 128, :], yb[:, tl, :], sem_os[tl], reads=[B_y[tl]])
                    out_evs.append(ev_st)
                    BLK.setdefault(f"st{fb}", []).append(ev_st)
            P.mark(f'f{fb}.final')
            if fb == 0:
                for b_ in range(2):
                    P.dma("sp", x1p[:, b_, :], x1s[(6 + b_) * 128:(7 + b_) * 128, :], sem_p[b_], writes=[B_x1p_g[b_]])
                P.barrier(skip=sem_os + sem_p)
            else:
                P.barrier()
        for fb in range(2):
            ffn_block(fb)

    if "modF" in dbg:
        fin.append(P.dma("sp", dbg["modF"], modF_t.rearrange("p a b c -> p (a b c)"), sem_o, reads=[B_modF]))
    if "gag" in dbg:
        fin.append(P.dma("sp", dbg["gag"], gag_t.rearrange("p a b -> p (a b)"), sem_o, reads=[B_gag]))
    P.wait_only("sp", fin + out_evs)
    print("recorded ops", P.nops, {k: len(v_) for k, v_ in P.ops.items()})
    if os.environ.get("KERNEL_MARKS"):
        import json as _json
        _json.dump(P.marks, open(os.environ["KERNEL_MARKS"], "w"))
    P.emit()
    return nc, list(dts.keys())


def _dft_tables(T, blocks):
    n = T // blocks
    idx = np.arange(n)
    ang = 2.0 * np.pi * ((idx[:, None] * idx[None, :]) % n) / n
    c = np.cos(ang) / np.sqrt(n)
    s = -np.sin(ang) / np.sqrt(n)
    ct = np.zeros((T, T), np.float32)
    st = np.zeros((T, T), np.float32)
    for b in range(blocks):
        ct[b * n:(b + 1) * n, b * n:(b + 1) * n] = c
        st[b * n:(b + 1) * n, b * n:(b + 1) * n] = s
    return ct.astype(ml_dtypes.bfloat16), st.astype(ml_dtypes.bfloat16)


def _consts():
    bf = ml_dtypes.bfloat16
    i = np.arange(128)
    same = (i[:, None] // 64) == (i[None, :] // 64)
    Mf = (same & (i[:, None] <= i[None, :])).astype(np.float32)
    Mb = (same & (i[:, None] >= i[None, :])).astype(np.float32)
    I = np.eye(128, dtype=np.float32)
    cum = np.zeros((128, 2, 256), np.float32)
    cum[:, 0, :128] = -Mf / 16.0
    cum[:, 0, 128:] = -(Mb - I) / 16.0
    cum[:, 1, :128] = -Mb / 16.0
    cum[:, 1, 128:] = -(Mf - I) / 16.0
    amask = np.stack([Mf, Mb], axis=1)
    c = np.arange(256)
    ang = 2.0 * np.pi * ((c[:, None] * c[None, :]) % 256) / 256.0
    cs = np.concatenate([np.cos(ang) / 16.0, np.sin(ang) / 16.0], axis=1).astype(bf)
    return dict(cum=cum, amask=np.ascontiguousarray(amask), cs=cs, ident=np.eye(128, dtype=np.float32).astype(bf))


def _fm(v, n):
    return np.ascontiguousarray(np.asarray(v, np.float32).reshape(n, 128).T)


def make_in_maps(inp):
    g = lambda k: np.asarray(inp[k])
    x_prompt, x_sample = g("x_prompt"), g("x_sample")
    c, c_ctx = g("c"), g("c_ctx")
    sf, sb = g("state_gla_fwd"), g("state_gla_bwd")
    consts = _consts()
    ctA1, nstA1 = _dft_tables(1024, 1)
    ctA4, nstA4 = _dft_tables(1024, 4)
    ctB, nstB = _dft_tables(512, 2)
    a_fb = np.zeros((33, 2, 512), np.float32)
    a_fb[0:16, 0] = g("w_a2_fwd")[0]
    a_fb[32, 0] = g("b_a_fwd")[0]
    a_fb[16:32, 1] = g("w_a2_bwd")[0]
    a_fb[32, 1] = g("b_a_bwd")[0]
    shared = dict(
        ctB=ctB, nstB=nstB, **consts,
        w_ada=g("w_ada")[0], b_adal=_fm(g("b_ada")[0], 96), b_ada=g("b_ada"),
        gprel=np.concatenate([_fm(g("norm_pre_mix")[0], 16), _fm(g("norm_pre_ffn")[0], 16)], axis=1),
        gpost=np.stack([g("norm_post_mix")[0], g("norm_post_ffn")[0]], axis=0),
        w_in=g("w_in")[0], a_fb=a_fb, gn=g("gla_out_norm").reshape(1, 1024),
        w_br_gla=g("w_br_gla")[0], w_br_four=g("w_br_four")[0], w_gate=g("w_gate")[0],
        b_gatel=_fm(g("b_gate")[0], 32), w_out=g("w_out")[0],
        w_ffn_gate=g("w_ffn_gate")[0], w_ffn_up=g("w_ffn_up")[0], w_ffn_down=g("w_ffn_down")[0],
    )
    maps = []
    plan = []
    zero_state = np.zeros((4, 128, 256), np.float32)
    for core in range(8):
        if core < 4:
            big = x_sample[core]
            pr = [2 * core, 2 * core + 1]
            conds = np.stack([c[core], c_ctx], axis=0)
            s0f_, s0b_ = sf[core, 0], sb[core, 0]
            lk = 1.0
            cta, nsta = ctA1, nstA1
            segs = [None, None, None, None, pr[0], pr[1]]
        else:
            base = 8 + 6 * (core - 4)
            big = x_prompt[base:base + 4].reshape(1024, D)
            pr = [base + 4, base + 5]
            conds = np.stack([c_ctx, c_ctx], axis=0)
            s0f_, s0b_ = zero_state, zero_state
            lk = 0.0
            cta, nsta = ctA4, nstA4
            segs = [base, base + 1, base + 2, base + 3, base + 4, base + 5]
        xin = np.concatenate([big, x_prompt[pr[0]], x_prompt[pr[1]]], axis=0)
        condl = np.zeros((128, 32), np.float32)
        for j in range(2):
            condl[:, j::2] = _fm(conds[j], 16)
        linkv = np.zeros((128, 2), np.float32)
        linkv[:, 0] = lk
        m = dict(shared)
        m.update(xin=np.ascontiguousarray(xin, dtype=np.float32), condl=condl,
                 s0f=np.ascontiguousarray(s0f_, dtype=np.float32), s0b=np.ascontiguousarray(s0b_, dtype=np.float32),
                 link=linkv, ctA=cta, nstA=nsta)
        maps.append(m)
        plan.append(segs)
    return maps, plan


_NC_CACHE = {}


def kernel(**inputs):
    maps, plan = make_in_maps(inputs)
    if "nc" not in _NC_CACHE:
        _NC_CACHE["nc"] = build_nc()
    nc, names = _NC_CACHE["nc"]
    maps = [{k: m[k] for k in names} for m in maps]
    res = bass_utils.run_bass_kernel_spmd(nc, maps, core_ids=list(range(8)))
    y_prompt = np.zeros((32, 256, D), np.float32)
    y_sample = np.zeros((4, 1024, D), np.float32)
    nsf = np.zeros((32, 1, 4, 128, 256), np.float32)
    nsb = np.zeros((32, 1, 4, 128, 256), np.float32)
    for core in range(8):
        r = res.results[core]
        yo = r["yout"]
        segs = plan[core]
        if core < 4:
            y_sample[core] = yo[0:1024]
        for s, pid in enumerate(segs):
            if pid is None:
                continue
            y_prompt[pid] = yo[s * 256:(s + 1) * 256]
            nsf[pid, 0] = r["snf"][s]
            nsb[pid, 0] = r["snb"][s]
    return (y_prompt, y_sample, nsf, nsb)
```

```python
import os
import numpy as np
import ml_dtypes
import concourse.bass as bass
import concourse.mybir as mybir
import concourse.bass_utils as bass_utils

F32 = mybir.dt.float32
BF16 = mybir.dt.bfloat16
AF = mybir.ActivationFunctionType
ALU = mybir.AluOpType

D = 2048
NTOK = 1536
NT = 12
HID = 5632
EPS = 1e-6
IN_COLS = 4128
PASSES = [(0, 8), (8, 4)]


class Ev:
    __slots__ = ("sem", "val")

    def __init__(self, sem, val):
        self.sem = sem
        self.val = val


class Buf:
    __slots__ = ("w", "r", "name", "excl")

    def __init__(self, name="", excl=False):
        self.w = []
        self.r = []
        self.name = name
        self.excl = excl


class DmaSem:
    def __init__(self, nc, name):
        self.sem = nc.alloc_semaphore(name=name)
        self.count = 0
        self.in_barrier = True


class Prog:
    ENGS = ("pe", "act", "dve", "pool", "sp")

    def __init__(self, nc):
        self.nc = nc
        self.ops = {e: [] for e in self.ENGS}
        self.sem = {e: nc.alloc_semaphore(name=f"q_{e}") for e in self.ENGS}
        self.cnt = {e: 0 for e in self.ENGS}
        self.nsem = 0
        self.nops = 0
        self.dsems = []
        self.marks = []

    def dsem(self, name):
        self.nsem += 1
        d = DmaSem(self.nc, f"d_{name}_{self.nsem}")
        self.dsems.append(d)
        return d

    def mark(self, name):
        self.marks.append((name, dict(self.cnt)))

    def barrier(self, skip=()):
        evs = [Ev(self.sem[x], self.cnt[x]) for x in self.ENGS if self.cnt[x] > 0]
        evs += [Ev(d.sem, d.count) for d in self.dsems if d.count > 0 and d.in_barrier and d not in skip]
        for e in self.ENGS:
            self.ops[e].append((None, list(evs), None, 0))

    def _deps(self, reads, writes, wacc, extra):
        waits = list(extra)
        for b in reads:
            waits += b.w
            if b.excl:
                waits += b.r
        for b in list(writes) + list(wacc):
            waits += b.w
            waits += b.r
        return [w for w in waits if w is not None]

    def _update(self, ev, reads, writes, wacc):
        for b in reads:
            b.r.append(ev)
        for b in writes:
            b.w = [ev]
            b.r = []
        for b in wacc:
            b.w.append(ev)

    def op(self, eng, fn, reads=(), writes=(), wacc=(), extra=()):
        waits = self._deps(reads, writes, wacc, extra)
        self.cnt[eng] += 1
        ev = Ev(self.sem[eng], self.cnt[eng])
        self.ops[eng].append((fn, waits, ev, 1))
        self._update(ev, reads, writes, wacc)
        self.nops += 1
        return ev

    def dma(self, eng, out_ap, in_ap, dsem, reads=(), writes=(), wacc=(), extra=()):
        waits = self._deps(reads, writes, wacc, extra)
        dsem.count += 16
        ev = Ev(dsem.sem, dsem.count)
        fn = (lambda e, o=out_ap, i=in_ap: e.dma_start(out=o, in_=i))
        self.ops[eng].append((fn, waits, ev, 16))
        self._update(ev, reads, writes, wacc)
        return ev

    def wait_only(self, eng, waits):
        self.ops[eng].append((None, [w for w in waits if w is not None], None, 0))

    def emit(self):
        nc = self.nc
        with nc.Block() as block:
            def run(engname):
                def body(e):
                    waited = {}
                    for fn, waits, ev, inc in self.ops[engname]:
                        best = {}
                        for w in waits:
                            k = id(w.sem)
                            if waited.get(k, 0) >= w.val:
                                continue
                            if k not in best or best[k].val < w.val:
                                best[k] = w
                        for k, w in best.items():
                            waited[k] = w.val
                            e.wait_ge(w.sem, w.val)
                        if fn is None:
                            continue
                        ins = fn(e)
                        if ev is not None:
                            ins.then_inc(ev.sem, inc)
                return body
            block.tensor(run("pe"))
            block.scalar(run("act"))
            block.vector(run("dve"))
            block.gpsimd(run("pool"))
            block.sync(run("sp"))


class Arena:
    def __init__(self, nc, nbytes):
        self.nc = nc
        self.t = nc.alloc_sbuf_tensor("arena", [128, nbytes], mybir.dt.uint8)
        self.nbytes = nbytes

    def view(self, off, shape, dtype):
        esz = 2 if dtype == BF16 else 4
        n = 1
        for s in shape[1:]:
            n *= s
        assert off % 4 == 0 and off + n * esz <= self.nbytes, (off, shape, self.nbytes)
        ap = self.t[:, off:off + n * esz].bitcast(dtype)
        if len(shape) > 2:
            names = " ".join(f"d{i}" for i in range(len(shape) - 1))
            kw = {f"d{i}": shape[i + 1] for i in range(len(shape) - 1)}
            ap = ap.rearrange(f"p ({names}) -> p {names}", **kw)
        return ap


class Ring:
    def __init__(self, prog, arena, off, nslots, slot_bytes):
        self.p = prog
        self.arena = arena
        self.off = off
        self.n = nslots
        self.sb = slot_bytes
        self.sems = [prog.dsem(f"ring{i}") for i in range(nslots)]
        for d_ in self.sems:
            d_.in_barrier = False
        self.bufs = [Buf(f"ring{i}") for i in range(nslots)]
        self.pending = []
        self.keys = {}
        self.next_fill = 0
        self.released = 0

    def add(self, key, dmas):
        self.keys[key] = len(self.pending)
        self.pending.append(dmas)

    def slot_view(self, slot, shape, dtype, boff=0):
        return self.arena.view(self.off + slot * self.sb + boff, shape, dtype)

    def _pump(self):
        while self.next_fill < len(self.pending) and self.next_fill < self.released + self.n:
            idx = self.next_fill
            slot = idx % self.n
            first = True
            for (eng, mk) in self.pending[idx]:
                out_ap, in_ap = mk(slot)
                if first:
                    self.p.dma(eng, out_ap, in_ap, self.sems[slot], writes=[self.bufs[slot]])
                else:
                    self.p.dma(eng, out_ap, in_ap, self.sems[slot], wacc=[self.bufs[slot]])
                first = False
            self.next_fill += 1

    def acquire(self, key):
        idx = self.keys[key]
        self._pump()
        assert idx == self.released and idx < self.next_fill, (key, idx, self.next_fill, self.released)
        slot = idx % self.n
        return slot, self.bufs[slot]

    def acquire2(self, key):
        idx = self.keys[key]
        self._pump()
        assert idx == self.released + 1 and idx < self.next_fill, (key, idx, self.next_fill, self.released)
        slot = idx % self.n
        return slot, self.bufs[slot]

    def release(self, key):
        idx = self.keys[key]
        assert idx == self.released, (key, idx, self.released)
        self.released += 1
        self._pump()


def build_nc(stage=99, debug=None):
    nc = bass.Bass("TRN2", target_bir_lowering=False)
    KB = 1024
    SHAPES = dict(
        xin=([NTOK, D], F32), condl=([128, 32], F32), s0f=([4, 128, 256], F32), s0b=([4, 128, 256], F32),
        link=([128, 2], F32), ctA=([1024, 1024], BF16), nstA=([1024, 1024], BF16), ctB=([512, 512], BF16),
        nstB=([512, 512], BF16), cs=([256, 512], BF16), cum=([128, 2, 256], F32), amask=([128, 2, 128], F32),
        ident=([128, 128], BF16), w_ada=([D, 6 * D], F32), b_adal=([128, 96], F32), b_ada=([1, 6 * D], F32),
        gprel=([128, 32], F32), gpost=([2, D], F32), w_in=([D, IN_COLS], F32), a_fb=([33, 2, 512], F32),
        gn=([1, 1024], F32), w_br_gla=([1024, D], F32), w_br_four=([1024, D], F32), w_gate=([D, 2 * D], F32),
        b_gatel=([128, 32], F32), w_out=([D, D], F32), w_ffn_gate=([D, HID], F32), w_ffn_up=([D, HID], F32),
        w_ffn_down=([HID, D], F32))
    dts = {}

    def DI(name):
        if name not in dts:
            shp, dty = SHAPES[name]
            dts[name] = nc.dram_tensor(name, list(shp), dty, kind="ExternalInput").ap()
        return dts[name]

    def dout(name, shape, dtype=F32):
        return nc.dram_tensor(name, list(shape), dtype, kind="ExternalOutput").ap()

    yout = dout("yout", [NTOK, D])
    snf = dout("snf", [6, 4, 128, 256])
    snb = dout("snb", [6, 4, 128, 256])
    x1s = nc.dram_tensor("x1s", [NTOK, D], F32, kind="Internal").ap()
    dbg = {}
    if debug:
        for name, shape in debug.items():
            dbg[name] = dout("dbg_" + name, shape)

    P = Prog(nc)
    AR = Arena(nc, 206 * KB)
    PS = [nc.alloc_psum_tensor(f"psb{i}", [128, 512], F32) for i in range(8)]
    PSB = [Buf(f"psb{i}", excl=True) for i in range(8)]
    PS0h = [Buf("ps0a"), Buf("ps0b")]

    def psv(i, shape, dtype=F32, boff=0):
        esz = 2 if dtype == BF16 else 4
        n = 1
        for s in shape[1:]:
            n *= s
        assert boff % 4 == 0 and boff + n * esz <= 2048
        ap = PS[i][:, boff // 4:(boff + n * esz) // 4]
        if dtype == BF16:
            ap = ap.bitcast(BF16)
        if len(shape) > 2:
            names = " ".join(f"d{i}" for i in range(len(shape) - 1))
            kw = {f"d{i}": shape[i + 1] for i in range(len(shape) - 1)}
            ap = ap.rearrange(f"p ({names}) -> p {names}", **kw)
        return ap

    off = 0

    def take(n):
        nonlocal off
        o = off
        off += (n + 31) // 32 * 32
        return o
    ident_t = AR.view(take(256), [128, 128], BF16)
    amask_t = AR.view(take(1024), [128, 2, 128], F32)
    cum_t = AR.view(take(2048), [128, 2, 256], F32)
    cs_t = AR.view(take(2048), [128, 2, 512], BF16)
    condl_t = AR.view(take(128), [128, 32], F32)
    csb_t = AR.view(take(64), [128, 32], BF16)
    badal_t = AR.view(take(384), [128, 96], F32)
    gprel_t = AR.view(take(128), [128, 32], F32)
    bgatel_t = AR.view(take(128), [128, 32], F32)
    link_t = AR.view(take(8), [128, 2], F32)
    modF_t = AR.view(take(512), [128, 4, 16, 2], F32)
    stat_t = AR.view(take(4 * 256), [128, 256], F32)
    part_t = AR.view(take(4 * 96), [128, 96], F32)
    a_fb_t = AR.view(take(2 * 512 * 4), [128, 2, 512], F32)
    gag_t = AR.view(take(16 * KB), [128, 2, D], F32)
    O_RING = off
    RING_SLOT = 16 * KB
    NSLOT = 3
    off += NSLOT * RING_SLOT
    O_BIG = off
    BIGSZ = AR.nbytes - O_BIG
    print("arena const+ring bytes", off, "big region", BIGSZ)
    ring = Ring(P, AR, O_RING, NSLOT, RING_SLOT)

    sem_c = P.dsem("const")
    sem_m = P.dsem("mods")
    sem_m2 = P.dsem("mods2")
    B_const = Buf("const")
    B_stat = Buf("stat")

    for t_, nm in ((ident_t, "ident"), (amask_t, "amask"), (cum_t, "cum"), (condl_t, "condl"), (badal_t, "b_adal"),
                   (gprel_t, "gprel"), (bgatel_t, "b_gatel"), (link_t, "link")):
        P.dma("sp", t_, DI(nm), sem_c, wacc=[B_const])
    P.dma("sp", cs_t, DI("cs").rearrange("(c p) n -> p c n", p=128), sem_c, wacc=[B_const])
    P.dma("sp", a_fb_t[0:33], DI("a_fb"), sem_c, wacc=[B_const])
    P.op("dve", lambda e: e.memset(stat_t, 0.0), writes=[B_stat])
    B_part = Buf("part")
    P.op("dve", lambda e: e.memset(part_t, 0.0), writes=[B_part])
    B_csb = Buf("csb")
    P.op("act", lambda e: e.activation(out=csb_t, in_=condl_t, func=AF.Silu), reads=[B_const], writes=[B_csb])

    def wtile(src, r0, nk, c0, ncols, boff=0, tot=None, eng="pool"):
        tot_ = tot or ncols

        def mk(slot):
            o = ring.slot_view(slot, [128, nk, tot_], BF16)[:, :, boff:boff + ncols]
            i = src[r0:r0 + nk * 128, c0:c0 + ncols].rearrange("(k p) c -> p k c", p=128)
            return o, i
        return (eng, mk)

    def wdma(src_ap, view_fn, eng="pool"):
        def mk(slot):
            return view_fn(slot), src_ap
        return (eng, mk)

    def rows_pk(src, r0, nk, c0, ncols):
        return src[r0:r0 + nk * 128, c0:c0 + ncols].rearrange("(k p) c -> p k c", p=128)

    w_ada = DI("w_ada")
    for c in range(8 if stage >= 3 else 12):
        ring.add(("ada", c), [wtile(w_ada, 0, 16, c * 512, 512)])
    if stage >= 2:
        w_in = DI("w_in")
        for ps_ in range(2):
            ring.add(("lr", ps_), [wtile(w_in, 0, 16, 3072, 32)])
            def add_qk(h):
                ring.add(("qk", ps_, h), [wtile(w_in, 0, 16, h * 128, 128, 0, 256),
                                          wtile(w_in, 0, 16, 512 + h * 128, 128, 128, 256)])

            def add_vg(h):
                ring.add(("vg", ps_, h), [wtile(w_in, 0, 16, 1024 + h * 256, 256, 0, 512),
                                          wtile(w_in, 0, 16, 2048 + h * 256, 256, 256, 512)])
            add_qk(0)
            add_vg(0)
            for h in range(4):
                if h + 1 < 4:
                    add_vg(h + 1)
                if stage >= 4 and ps_ == 1 and h < 2:
                    for c in range(4):
                        ring.add(("ada", 12 + 4 * h + c), [wtile(w_ada, 0, 16, 6144 + (4 * h + c) * 512, 512)])
                if h + 1 < 4:
                    add_qk(h + 1)
            if stage >= 3 and ps_ == 0:
                for c in range(8, 12):
                    ring.add(("ada", c), [wtile(w_ada, 0, 16, c * 512, 512)])
            if stage >= 3:
                ntp_ = PASSES[ps_][1]
                T_ = ntp_ * 128
                ct_, nst_ = (DI("ctA"), DI("nstA")) if ps_ == 0 else (DI("ctB"), DI("nstB"))
                for g in range(4):
                    ring.add(("fo", ps_, g), [wtile(w_in, 0, 16, 3104 + g * 256, 256)])
                    for q5 in range(T_ // 512):
                        ring.add(("tab", ps_, g, q5), [
                            wdma(rows_pk(ct_, 0, ntp_, q5 * 512, 512),
                                 lambda slot, ntp_=ntp_: ring.slot_view(slot, [128, 2, ntp_, 512], BF16)[:, 0, :, :]),
                            wdma(rows_pk(nst_, 0, ntp_, q5 * 512, 512),
                                 lambda slot, ntp_=ntp_: ring.slot_view(slot, [128, 2, ntp_, 512], BF16)[:, 1, :, :])])
                for jp in range(8):
                    ring.add(("gate", ps_, jp), [
                        wdma(rows_pk(DI("w_gate"), 0, 16, jp * 256, 256),
                             lambda slot: ring.slot_view(slot, [128, 2, 16, 256], BF16)[:, 0, :, :]),
                        wdma(rows_pk(DI("w_gate"), 0, 16, 2048 + jp * 256, 256),
                             lambda slot: ring.slot_view(slot, [128, 2, 16, 256], BF16)[:, 1, :, :])])
                    ring.add(("br", ps_, jp), [
                        wdma(rows_pk(DI("w_br_gla"), 0, 8, jp * 256, 256),
                             lambda slot: ring.slot_view(slot, [128, 2, 8, 256], BF16)[:, 0, :, :]),
                        wdma(rows_pk(DI("w_br_four"), 0, 8, jp * 256, 256),
                             lambda slot: ring.slot_view(slot, [128, 2, 8, 256], BF16)[:, 1, :, :])])
                for c in range(4):
                    ring.add(("wo", ps_, c), [wtile(DI("w_out"), 0, 16, c * 512, 512)])
    if stage >= 4:
        for c in range(8, 12):
            ring.add(("ada", 12 + c), [wtile(w_ada, 0, 16, 6144 + c * 512, 512)])
        for fb in range(2):
            for hp in range(22):
                ring.add(("f1", fb, hp), [
                    wdma(rows_pk(DI("w_ffn_gate"), 0, 16, hp * 256, 256),
                         lambda slot: ring.slot_view(slot, [128, 2, 16, 256], BF16)[:, 0, :, :]),
                    wdma(rows_pk(DI("w_ffn_up"), 0, 16, hp * 256, 256),
                         lambda slot: ring.slot_view(slot, [128, 2, 16, 256], BF16)[:, 1, :, :])])
            for c in range(4):
                for pz in range(4):
                    ring.add(("f2", fb, c, pz), [wtile(DI("w_ffn_down"), pz * 1408, 11, c * 512, 512)])

    MT = {}

    def set_mods_tmp(base):
        MT["csrep"] = AR.view(base, [128, 16, 2, 128], BF16)
        MT["bga"] = AR.view(base + 8 * KB, [128, D], F32)
        MT["gpo"] = AR.view(base + 16 * KB, [128, D], F32)
    set_mods_tmp(O_BIG)
    B_csrep, B_bga, B_gpo, B_gag, B_modF = Buf(), Buf(), Buf(), Buf(), Buf()

    def mod_feature(key, mslot, csel):
        slot, wb = ring.acquire(key)
        wv = ring.slot_view(slot, [128, 16, 512], BF16)

        def fn(e):
            ins = None
            for q in range(4):
                ch = csel * 4 + q
                for k in range(16):
                    ins = e.matmul(psv(0, [128, 64, 2])[:, mslot * 16 + ch, :], lhsT=wv[:, k, q * 128:(q + 1) * 128],
                                   rhs=csb_t[:, 2 * k:2 * k + 2], start=(k == 0), stop=(k == 15))
            return ins
        P.op("pe", fn, reads=[wb, B_csb], wacc=[PSB[0]])
        ring.release(key)

    def evac_mod(m, ada_idx):
        P.op("dve", lambda e: e.tensor_tensor(
            out=modF_t[:, m, :, :], in0=psv(0, [128, 64, 2])[:, (m % 2) * 16:(m % 2 + 1) * 16, :],
            in1=badal_t[:, ada_idx * 16:(ada_idx + 1) * 16].unsqueeze(2).to_broadcast([128, 16, 2]),
            op=ALU.add), reads=[PSB[0], B_const], wacc=[B_modF])

    def mod_scale(m, goff):
        P.op("dve", lambda e: e.scalar_tensor_tensor(
            out=modF_t[:, m, :, :], in0=modF_t[:, m, :, :], scalar=1.0,
            in1=gprel_t[:, goff:goff + 16].unsqueeze(2).to_broadcast([128, 16, 2]),
            op0=ALU.add, op1=ALU.mult), reads=[B_const], writes=[B_modF])

    def mod_bcast(key, c):
        csrep_t, bga_t, gpo_t = MT["csrep"], MT["bga"], MT["gpo"]
        slot, wb = ring.acquire(key)
        wv = ring.slot_view(slot, [128, 16, 512], BF16)
        for j in range(2):
            def fn(e, j=j):
                ins = None
                for k in range(16):
                    ins = e.matmul(psv(1 + j, [128, 512]), lhsT=csrep_t[:, k, j, :], rhs=wv[:, k, :],
                                   start=(k == 0), stop=(k == 15))
                return ins
            P.op("pe", fn, reads=[wb, B_csrep], writes=[PSB[1 + j]])
        ring.release(key)
        for j in range(2):
            P.op("dve", lambda e, j=j: e.tensor_tensor(
                out=gag_t[:, j, c * 512:(c + 1) * 512], in0=psv(1 + j, [128, 512]),
                in1=bga_t[:, c * 512:(c + 1) * 512], op=ALU.add), reads=[PSB[1 + j], B_bga], wacc=[B_gag])
            P.op("dve", lambda e, j=j: e.tensor_tensor(
                out=gag_t[:, j, c * 512:(c + 1) * 512], in0=gag_t[:, j, c * 512:(c + 1) * 512],
                in1=gpo_t[:, c * 512:(c + 1) * 512], op=ALU.mult), reads=[B_gpo], writes=[B_gag])

    def mods_feat(which, grp):
        base = 12 * which + 4 * grp
        for c in range(4):
            mod_feature(("ada", base + c), grp, c)
        evac_mod(2 * which + grp, 3 * which + grp)
        if grp == 1:
            mod_scale(2 * which + 1, 16 * which)

    def mods_bcast(which):
        base = 12 * which
        csrep_t, bga_t, gpo_t = MT["csrep"], MT["bga"], MT["gpo"]
        P.op("dve", lambda e: e.tensor_copy(out=csrep_t.rearrange("p k j m -> p (k j) m"),
                                            in_=csb_t.unsqueeze(2).to_broadcast([128, 32, 128])),
             reads=[B_csb], writes=[B_csrep])
        P.dma("sp", bga_t, DI("b_ada")[0:1, (2 + 3 * which) * D:(3 + 3 * which) * D].to_broadcast([128, D]),
              sem_m, writes=[B_bga])
        P.dma("sp", gpo_t, DI("gpost")[which:which + 1, :].to_broadcast([128, D]), sem_m2, writes=[B_gpo])
        for c in range(4):
            mod_bcast(("ada", base + 8 + c), c)

    mods_feat(0, 0)
    mods_feat(0, 1)
    P.mark('mods0')

    def pass_layout(ntp):
        T = ntp * 128
        o = O_BIG
        L = {}

        def tk(name, shape, dtype):
            nonlocal o
            esz = 2 if dtype == BF16 else 4
            n = 1
            for s_ in shape[1:]:
                n *= s_
            L[name] = AR.view(o, shape, dtype)
            o += (n * esz + 31) // 32 * 32
        tk("hT", [128, 16, T], BF16)
        tk("oT", [128, 8, T], BF16)
        L["_tmp0"] = o
        tk("xbuf", [128, 3, D], F32)
        tk("xn", [128, 2, D], BF16)
        tk("tmpa", [128, 2, 8, 128], F32)
        tk("junkp", [128, D], BF16)
        o = L["_tmp0"]
        tk("lrT", [128, T], F32)
        tk("nla", [128, ntp, 2, 128], F32)
        L["zt"] = AR.view(o, [128, ntp, 2, 128], F32)
        tk("E", [128, 2, 3, 512], F32)
        tk("qin", [128, 2, T], BF16)
        tk("kin", [128, 2, T], BF16)
        tk("koT", [128, 2, 512], BF16)
        tk("ko", [128, 2, ntp, 128], BF16)
        tk("v", [128, 2, ntp, 256], BF16)
        tk("sgn", [128, 2, ntp, 256], BF16)
        tk("sgt", [128, 2, 256], F32)
        tk("gnb", [128, 2, 256], F32)
        tk("snapF", [128, 2 * ntp + 1, 256], BF16)
        tk("snapB", [128, 2 * ntp + 1, 256], BF16)
        tk("S", [128, 2, 2, 256], F32)
        tk("dec", [128, 2, 2 * ntp], F32)
        tk("attT", [128, 2, 2, 128], BF16)
        tk("og", [128, 2, 256], BF16)
        assert o <= AR.nbytes, (o, AR.nbytes)
        print("pass layout ntp", ntp, "end", o, "of", AR.nbytes)
        return L

    sem_x = [P.dsem("x0"), P.dsem("x1"), P.dsem("x2")]
    sem_o = P.dsem("out")
    sem_so = [[P.dsem("so00"), P.dsem("so01")], [P.dsem("so10"), P.dsem("so11")]]
    sem_si = [P.dsem("stin0"), P.dsem("stin1")]
    sem_g = [P.dsem("gn0"), P.dsem("gn1")]
    out_evs = []
    x1_evs = []
    fin = []
    sem_x1 = P.dsem("x1st")
    sem_xr = P.dsem("x3")
    sem_os = [P.dsem(f"os{i}") for i in range(6)]
    sem_f = [P.dsem(f"x1f{i}") for i in range(6)]
    B_st2 = Buf("st2")
    SC = 128 ** -0.5
    LNSC = float(np.log(SC))

    def run_pass(pidx):
        t0, ntp = PASSES[pidx]
        T = ntp * 128
        n5 = T // 512
        nch = 2 * ntp
        j = pidx
        L = pass_layout(ntp)
        if pidx > 0:
            P.barrier()
        hT, oT = L["hT"], L["oT"]
        B_hT = [Buf(f"hT{t}") for t in range(ntp)]
        B_oT = [Buf(f"oT{t}") for t in range(ntp)]
        xbuf, xn, tmpa, junkp = L["xbuf"], L["xn"], L["tmpa"], L["junkp"]
        B_xb = [Buf(), Buf(), Buf()]
        B_xn = [Buf(), Buf()]
        B_tmpa = [Buf(), Buf()]
        B_junk0 = Buf()
        B_ss = [Buf() for _ in range(ntp)]
        B_rs = [Buf() for _ in range(ntp)]

        def p2_front(tl):
            t = t0 + tl
            b = tl % 3
            P.dma("sp", xbuf[:, b, :], DI("xin")[t * 128:(t + 1) * 128, :], sem_x[b], writes=[B_xb[b]])
            P.op("act", lambda e: e.activation(out=junkp, in_=xbuf[:, b, :], func=AF.Square, accum_out=stat_t[:, t:t + 1]),
                 reads=[B_xb[b], B_stat], writes=[B_junk0, B_ss[tl]])
            P.op("dve", lambda e: e.tensor_scalar(out=stat_t[:, 16 + t:17 + t], in0=stat_t[:, t:t + 1],
                                                  scalar1=1.0 / D, scalar2=EPS, op0=ALU.mult, op1=ALU.add),
                 reads=[B_ss[tl]], writes=[B_rs[tl]])
            P.op("dve", lambda e: e.reciprocal(out=stat_t[:, 16 + t:17 + t], in_=stat_t[:, 16 + t:17 + t]),
                 writes=[B_rs[tl]])

        def p2_back(tl):
            t = t0 + tl
            b = tl % 3
            nb_ = tl % 2
            P.op("act", lambda e: e.activation(out=stat_t[:, 16 + t:17 + t], in_=stat_t[:, 16 + t:17 + t], func=AF.Sqrt),
                 writes=[B_rs[tl]])
            P.op("act", lambda e: e.activation(out=xn[:, nb_, :], in_=xbuf[:, b, :], func=AF.Copy, scale=stat_t[:, 16 + t:17 + t]),
                 reads=[B_xb[b], B_rs[tl]], writes=[B_xn[nb_]])
            for half in range(2):
                bank = 6 + half

                def fn(e, half=half, bank=bank):
                    ins = None
                    for kk in range(8):
                        kc = half * 8 + kk
                        ins = e.transpose(out=psv(bank, [128, 8, 128], BF16)[:, kk, :],
                                          in_=xn[:, nb_, kc * 128:(kc + 1) * 128], identity=ident_t)
                    return ins
                P.op("pe", fn, reads=[B_xn[nb_], B_const], writes=[PSB[bank]])
                h8 = half * 8
                P.op("dve", lambda e, half=half, bank=bank, h8=h8: e.tensor_tensor(
                    out=tmpa[:, half, :, :], in0=psv(bank, [128, 8, 128], BF16),
                    in1=modF_t[:, 1, h8:h8 + 8, j:j + 1].to_broadcast([128, 8, 128]), op=ALU.mult),
                    reads=[PSB[bank], B_modF], writes=[B_tmpa[half]])
                P.op("dve", lambda e, half=half, h8=h8: e.tensor_tensor(
                    out=hT[:, h8:h8 + 8, tl * 128:(tl + 1) * 128], in0=tmpa[:, half, :, :],
                    in1=modF_t[:, 0, h8:h8 + 8, j:j + 1].to_broadcast([128, 8, 128]), op=ALU.add),
                    reads=[B_tmpa[half], B_modF], wacc=[B_hT[tl]])
        for tl in range(ntp + 1):
            if tl < ntp:
                p2_front(tl)
            if tl >= 1:
                p2_back(tl - 1)
        P.mark(f'p{pidx}.P2')
        if stage < 2:
            return L, B_hT, B_oT

        lrT, nla, zt, E = L["lrT"], L["nla"], L["zt"], L["E"]
        qin, kin, koT, ko, v2, sgn2, sgt, gnb2 = L["qin"], L["kin"], L["koT"], L["ko"], L["v"], L["sgn"], L["sgt"], L["gnb"]
        snapF, snapB, S, dec, attT, og = L["snapF"], L["snapB"], L["S"], L["dec"], L["attT"], L["og"]
        B_lrT = Buf("lrT")
        slot, wb = ring.acquire(("lr", pidx))
        wv = ring.slot_view(slot, [128, 16, 32], BF16)
        P.barrier()
        P.op("dve", lambda e: e.memset(lrT[32:33, :], 1.0), wacc=[B_lrT])
        for q5 in range(n5):
            def fn(e, q5=q5):
                ins = None
                for k in range(16):
                    ins = e.matmul(psv(0, [128, 512])[0:32, :], lhsT=wv[:, k, :], rhs=hT[:, k, q5 * 512:(q5 + 1) * 512],
                                   start=(k == 0), stop=(k == 15))
                return ins
            P.op("pe", fn, reads=[wb] + B_hT[q5 * 4:(q5 + 1) * 4], writes=[PSB[0]])
            P.op("dve", lambda e, q5=q5: e.tensor_copy(out=lrT[0:32, q5 * 512:(q5 + 1) * 512], in_=psv(0, [128, 512])[0:32, :]),
                 reads=[PSB[0]], wacc=[B_lrT])
        ring.release(("lr", pidx))
        P.mark(f'p{pidx}.lr')

        B_nla, B_E = Buf(), Buf()
        B_zt = [B_E, B_E]
        B_qin, B_kin, B_koT, B_ko, B_v2, B_sgn2, B_sgt, B_gnb2 = Buf(), Buf(), Buf(), Buf(), [Buf(), Buf()], [Buf(), Buf()], [Buf(), Buf()], [Buf(), Buf()]
        B_snapF, B_snapBa, B_S, B_dec, B_attT, B_og = Buf(), Buf(), [[Buf(), Buf()], [Buf(), Buf()]], Buf(), [Buf(), Buf()], [Buf(), Buf()]
        a_fb = a_fb_t
        def prepc_gen(hh):
            hq = hh % 2
            vv, ss_, Bv, Bs = v2[:, hq, :, :], sgn2[:, hq, :, :], B_v2[hq], B_sgn2[hq]
            gq = gnb2[:, hq, :]
            slot, wb = ring.acquire(("vg", pidx, hh))
            wvg = ring.slot_view(slot, [128, 16, 512], BF16)
            P.dma("sp", gq, DI("gn")[0:1, hh * 256:(hh + 1) * 256].to_broadcast([128, 256]), sem_g[hq], writes=[B_gnb2[hq]])
            for tl in range(ntp):
                vb = 4 + tl % 2

                def fn(e, tl=tl, vb=vb):
                    ins = None
                    for k in range(16):
                        ins = e.matmul(psv(vb, [128, 512]), lhsT=hT[:, k, tl * 128:(tl + 1) * 128], rhs=wvg[:, k, :],
                                       start=(k == 0), stop=(k == 15))
                    return ins
                P.op("pe", fn, reads=[wb, B_hT[tl]], writes=[PSB[vb]])
                sb_ = tl % 2
                P.op("act", lambda e, tl=tl, vb=vb: e.activation(out=vv[:, tl, :], in_=psv(vb, [128, 512])[:, 0:256], func=AF.Copy),
                     reads=[PSB[vb]], wacc=[Bv])
                P.op("act", lambda e, sb_=sb_, vb=vb: e.activation(out=sgt[:, sb_, :], in_=psv(vb, [128, 512])[:, 256:512], func=AF.Silu),
                     reads=[PSB[vb]], writes=[B_sgt[sb_]])
                P.op("pool", lambda e, sb_=sb_, tl=tl: e.tensor_tensor(out=ss_[:, tl, :], in0=sgt[:, sb_, :], in1=gq, op=ALU.mult),
                     reads=[B_sgt[sb_], B_gnb2[hq]], wacc=[Bs])
                yield
            ring.release(("vg", pidx, hh))

        GS = int(os.environ.get("GLA_STOP", "99"))

        def prepa(h):
            for pr in range(ntp // 2):
                bk = pr % 4

                def fn(e, pr=pr, bk=bk):
                    ins = None
                    for t2 in range(2):
                        tl = 2 * pr + t2
                        for d_ in range(2):
                            ins = e.matmul(psv(bk, [128, 2, 2, 128])[:, t2, d_, :], lhsT=lrT[0:33, tl * 128:(tl + 1) * 128],
                                           rhs=a_fb[0:33, d_, h * 128:(h + 1) * 128], start=True, stop=True)
                    return ins
                P.op("pe", fn, reads=[B_lrT, B_const], writes=[PSB[bk]])
                P.op("act", lambda e, pr=pr, bk=bk: e.activation(out=zt[:, 2 * pr:2 * pr + 2, :, :], in_=psv(bk, [128, 2, 2, 128]),
                                                                func=AF.Exp, scale=-1.0), reads=[PSB[bk]], wacc=[B_zt[0]])
            P.op("act", lambda e: e.activation(out=nla, in_=zt, func=AF.Ln, bias=1.0), reads=[B_zt[0]], writes=[B_nla])

        def do_head(h):
            if GS <= 0:
                return
            if h == 0:
                prepa(0)
            P.mark(f'p{pidx}.h{h}.prepa')
            if GS <= 1:
                return
            slot, wb = ring.acquire(("qk", pidx, h))
            wqk = ring.slot_view(slot, [128, 16, 256], BF16)
            for q5 in range(n5):
                for t4 in range(4):
                    tl = q5 * 4 + t4
                    cb = (1, 0, 5, 6)[tl % 4]

                    def fn(e, tl=tl, cb=cb):
                        ins = None
                        for d_ in range(2):
                            ins = e.matmul(psv(cb, [128, 2, 256])[:, d_, :], lhsT=nla[:, tl, d_, :], rhs=cum_t[:, d_, :],
                                           start=True, stop=True)
                        return ins
                    P.op("pe", fn, reads=[B_nla, B_const], writes=[PSB[cb]])
                    cs_ = slice(t4 * 128, (t4 + 1) * 128)
                    pc = psv(cb, [128, 2, 2, 128])
                    P.op("act", lambda e, cs_=cs_, pc=pc: e.activation(out=E[:, :, 0:3:2, cs_], in_=pc, func=AF.Exp),
                         reads=[PSB[cb]], wacc=[B_E])
                    P.op("act", lambda e, cs_=cs_, pc=pc: e.activation(out=E[:, :, 1, cs_], in_=pc[:, :, 0, :], func=AF.Exp, scale=-1.0),
                         reads=[PSB[cb]], wacc=[B_E])
                    c0_ = t4 * 128
                    P.op("dve", lambda e, tl=tl, c0_=c0_: e.tensor_copy(out=dec[:, 0, 2 * tl:2 * tl + 2], in_=E[:, 0, 0, c0_ + 63:c0_ + 128:64]),
                         reads=[B_E], wacc=[B_dec])
                    P.op("dve", lambda e, tl=tl, c0_=c0_: e.tensor_copy(out=dec[:, 1, 2 * tl:2 * tl + 2], in_=E[:, 1, 0, c0_:c0_ + 128:64]),
                         reads=[B_E], wacc=[B_dec])
                for qk in range(2):
                    def fn(e, qk=qk, q5=q5):
                        ins = None
                        for k in range(16):
                            ins = e.matmul(psv(2 + qk, [128, 512]), lhsT=wqk[:, k, qk * 128:(qk + 1) * 128],
                                           rhs=hT[:, k, q5 * 512:(q5 + 1) * 512], start=(k == 0), stop=(k == 15))
                        return ins
                    P.op("pe", fn, reads=[wb] + B_hT[q5 * 4:(q5 + 1) * 4], writes=[PSB[2 + qk]])
                c5 = slice(q5 * 512, (q5 + 1) * 512)
                for d_ in range(2):
                    P.op("dve", lambda e, d_=d_, c5=c5: e.scalar_tensor_tensor(out=qin[:, d_, c5], in0=psv(2, [128, 512]), scalar=SC,
                                                                               in1=E[:, d_, 0, :], op0=ALU.mult, op1=ALU.mult),
                         reads=[PSB[2], B_E], wacc=[B_qin])
                    P.op("dve", lambda e, d_=d_, c5=c5: e.tensor_tensor(out=kin[:, d_, c5], in0=psv(3, [128, 512]), in1=E[:, d_, 1, :], op=ALU.mult),
                         reads=[PSB[3], B_E], wacc=[B_kin])
                    P.op("dve", lambda e, d_=d_: e.tensor_tensor(out=koT[:, d_, :], in0=psv(3, [128, 512]), in1=E[:, d_, 2, :], op=ALU.mult),
                         reads=[PSB[3], B_E], wacc=[B_koT])
                def fn(e):
                    ins = None
                    for d_ in range(2):
                        for t4 in range(4):
                            ins = e.transpose(out=psv(4, [128, 2, 4, 128], BF16)[:, d_, t4, :], in_=koT[:, d_, t4 * 128:(t4 + 1) * 128],
                                              identity=ident_t)
                    return ins
                P.op("pe", fn, reads=[B_koT, B_const], writes=[PSB[4]])
                P.op("dve", lambda e, q5=q5: e.tensor_copy(out=ko[:, :, q5 * 4:(q5 + 1) * 4, :], in_=psv(4, [128, 2, 4, 128], BF16)),
                     reads=[PSB[4]], wacc=[B_ko])
            ring.release(("qk", pidx, h))
            P.mark(f'p{pidx}.h{h}.prepb')
            if GS <= 2:
                return
            hp = h % 2
            v, sgn, B_v, B_sgn = v2[:, hp, :, :], sgn2[:, hp, :, :], B_v2[hp], B_sgn2[hp]
            if h == 0:
                for _ in prepc_gen(0):
                    pass
            P.mark(f'p{pidx}.h{h}.prepc')
            if GS <= 3:
                return
            seg0 = 0 if pidx == 0 else 4
            lk = link_t[:, pidx:pidx + 1]

            def chain(d_, order, s0_ap, snap_of, out_t, kbanks):
                cur = 0
                if s0_ap is not None:
                    P.dma("sp", S[:, d_, cur, :], s0_ap, sem_si[d_], writes=[B_S[d_][cur]])
                else:
                    P.op("dve", lambda e, cur=cur: e.memset(S[:, d_, cur, :], 0.0), writes=[B_S[d_][cur]])
                sa, sbuf = snap_of(order[0])
                P.op("act", lambda e, sa=sa, cur=cur: e.activation(out=sa, in_=S[:, d_, cur, :], func=AF.Copy), reads=[B_S[d_][cur]], wacc=[sbuf])
                yield
                for ci, c in enumerate(order):
                    rows = slice((c % 2) * 64, (c % 2) * 64 + 64)
                    tl = c // 2
                    kb = kbanks[ci % 2]
                    P.op("pe", lambda e, rows=rows, tl=tl, kb=kb: e.matmul(psv(kb, [128, 2, 256])[:, d_, :], lhsT=ko[rows, d_, tl, :],
                                                                      rhs=v[rows, tl, :], start=True, stop=True),
                         reads=[B_ko, B_v], writes=[PSB[kb]])
                    nxt = 1 - cur
                    P.op("dve", lambda e, c=c, cur=cur, nxt=nxt, kb=kb: e.scalar_tensor_tensor(
                        out=S[:, d_, nxt, :], in0=S[:, d_, cur, :], scalar=dec[:, d_, c:c + 1], in1=psv(kb, [128, 2, 256])[:, d_, :],
                        op0=ALU.mult, op1=ALU.add), reads=[B_S[d_][cur], B_dec, PSB[kb]], writes=[B_S[d_][nxt]])
                    cur = nxt
                    last = (ci == len(order) - 1)
                    seg_end = last or ((c % 4 == 3) if d_ == 0 else (c % 4 == 0))
                    if seg_end:
                        seg = seg0 + c // 4
                        out_evs.append(P.dma("sp", out_t[seg, h], S[:, d_, cur, :], sem_so[d_][cur], reads=[B_S[d_][cur]]))
                    if not last:
                        cn = order[ci + 1]
                        sa, sbuf = snap_of(cn)
                        if seg_end:
                            nxt = 1 - cur
                            P.op("dve", lambda e, cur=cur, nxt=nxt: e.tensor_scalar_mul(out=S[:, d_, nxt, :], in0=S[:, d_, cur, :], scalar1=lk),
                                 reads=[B_S[d_][cur], B_const], writes=[B_S[d_][nxt]])
                            cur = nxt
                        P.op("act", lambda e, sa=sa, cur=cur: e.activation(out=sa, in_=S[:, d_, cur, :], func=AF.Copy), reads=[B_S[d_][cur]], wacc=[sbuf])
                    yield

            gf = chain(0, list(range(nch)), DI("s0f")[h] if pidx == 0 else None,
                       lambda c: (snapF[:, c, :], B_snapF), snf, (0, 3))
            gb = chain(1, list(range(nch - 1, -1, -1)), DI("s0b")[h] if pidx == 0 else None,
                       lambda c: (snapB[:, c, :], B_snapBa), snb, (1, 2))
            pg = prepc_gen(h + 1) if (h + 1 < 4 and GS >= 99) else iter(())
            for i_ in range(nch + 2):
                next(gf, None)
                next(gb, None)
                if i_ % 2 == 1:
                    next(pg, None)
            for _ in pg:
                pass
            if h + 1 < 4 and GS >= 99:
                prepa(h + 1)
            P.mark(f'p{pidx}.h{h}.fwd')
            if GS <= 4:
                return
            o_bank = lambda tl: 4 + tl // 2
            B_sso = Buf()

            def o_tile(tl):
                ab = tl % 2
                cs_ = slice(tl * 128, (tl + 1) * 128)

                atb = (1, 0)[tl % 2]

                def fn(e):
                    ins = None
                    for d_ in range(2):
                        ins = e.matmul(psv(atb, [128, 2, 128])[:, d_, :], lhsT=kin[:, d_, cs_], rhs=qin[:, d_, cs_], start=True, stop=True)
                    return ins
                P.op("pe", fn, reads=[B_kin, B_qin], writes=[PSB[atb]])
                P.op("dve", lambda e: e.tensor_tensor(out=attT[:, ab, :, :], in0=psv(atb, [128, 2, 128]), in1=amask_t, op=ALU.mult),
                     reads=[PSB[atb], B_const], writes=[B_attT[ab]])
                po = psv(o_bank(tl), [128, 2, 256])[:, tl % 2, :]

                def fn2(e):
                    e.matmul(po, lhsT=attT[:, ab, 0, :], rhs=v[:, tl, :], start=True, stop=False)
                    e.matmul(po, lhsT=attT[:, ab, 1, :], rhs=v[:, tl, :], start=False, stop=False)
                    ins = None
                    for cc in range(2):
                        c = 2 * tl + cc
                        rows = slice(cc * 64, cc * 64 + 64)
                        ccs = slice(c * 64, (c + 1) * 64)
                        e.matmul(po[rows, :], lhsT=qin[:, 0, ccs], rhs=snapF[:, c, :], start=False, stop=False)
                        ins = e.matmul(po[rows, :], lhsT=qin[:, 1, ccs], rhs=snapB[:, c, :], start=False, stop=True)
                    return ins
                P.op("pe", fn2, reads=[B_attT[ab], B_v, B_qin, B_snapF, B_snapBa], wacc=[PSB[o_bank(tl)]])
                col = 32 + h * 12 + t0 + tl
                P.op("act", lambda e: e.activation(out=og[:, 0, :], in_=po, func=AF.Square, accum_out=stat_t[:, col:col + 1]),
                     reads=[PSB[o_bank(tl)]], writes=[B_og[0]], wacc=[B_sso])
            for tl in range(ntp):
                o_tile(tl)
            P.mark(f'p{pidx}.h{h}.bwd')
            if GS <= 5:
                return
            c0_ = 32 + h * 12 + t0
            c1_ = 96 + h * 12 + t0
            P.op("dve", lambda e: e.tensor_scalar(out=stat_t[:, c1_:c1_ + ntp], in0=stat_t[:, c0_:c0_ + ntp], scalar1=1.0 / 256,
                                                  scalar2=EPS, op0=ALU.mult, op1=ALU.add), reads=[B_sso], writes=[B_sso])
            P.op("dve", lambda e: e.reciprocal(out=stat_t[:, c1_:c1_ + ntp], in_=stat_t[:, c1_:c1_ + ntp]), reads=[], writes=[B_sso])
            P.op("act", lambda e: e.activation(out=stat_t[:, c1_:c1_ + ntp], in_=stat_t[:, c1_:c1_ + ntp], func=AF.Sqrt),
                 reads=[], writes=[B_sso])
            for tl in range(ntp):
                ob = tl % 2
                po = psv(o_bank(tl), [128, 2, 256])[:, tl % 2, :]
                P.op("dve", lambda e, tl=tl, ob=ob, po=po: e.scalar_tensor_tensor(
                    out=og[:, ob, :], in0=po, scalar=stat_t[:, c1_ + tl:c1_ + tl + 1], in1=sgn[:, tl, :], op0=ALU.mult, op1=ALU.mult),
                    reads=[PSB[o_bank(tl)], B_sso, B_sgn], writes=[B_og[ob]])

                eb = 2 + tl % 2

                def fn(e, ob=ob, eb=eb):
                    ins = None
                    for vc in range(2):
                        ins = e.transpose(out=psv(eb, [128, 2, 128], BF16)[:, vc, :], in_=og[:, ob, vc * 128:(vc + 1) * 128], identity=ident_t)
                    return ins
                P.op("pe", fn, reads=[B_og[ob], B_const], writes=[PSB[eb]])
                P.op("act", lambda e, tl=tl, eb=eb: e.activation(out=oT[:, 2 * h:2 * h + 2, tl * 128:(tl + 1) * 128],
                                                         in_=psv(eb, [128, 2, 128], BF16), func=AF.Copy),
                     reads=[PSB[eb]], wacc=[B_oT[tl]])

        for h in range(4 if GS >= 99 else 1):
            do_head(h)
            if stage >= 4 and pidx == 1 and h < 2 and GS >= 99:
                mods_feat(1, h)
        if GS < 99:
            return L, B_hT, B_oT

        P.mark(f'p{pidx}.gla_done')
        if stage < 3:
            return L, B_hT, B_oT
        P.barrier()
        tb = L["_tmp0"]
        if pidx == 0:
            set_mods_tmp(tb)
            mods_bcast(0)
            set_mods_tmp(O_BIG)
            P.barrier()
        fT = AR.view(tb, [128, 8, T], BF16)
        foT = AR.view(tb + 16 * T, [128, 2, 2, T], BF16)
        Yt = AR.view(tb + 16 * T + 8 * T, [128, 2, ntp, 512], BF16)
        B_fT = [Buf() for _ in range(n5)]
        B_foT, B_Y = [Buf(), Buf()], [Buf(), Buf()]
        rot = {"a": 0, "b": 0, "c": 0}

        def nb(grp, banks):
            b_ = banks[rot[grp] % len(banks)]
            rot[grp] += 1
            return b_

        def do_group(g):
            gb = g % 2
            slot, wb = ring.acquire(("fo", pidx, g))
            wfo = ring.slot_view(slot, [128, 16, 256], BF16)
            for cc in range(2):
                for q5 in range(n5):
                    bk = nb("a", (0, 1, 2, 3))

                    def fn(e, cc=cc, q5=q5, bk=bk):
                        ins = None
                        for k in range(16):
                            ins = e.matmul(psv(bk, [128, 512]), lhsT=wfo[:, k, cc * 128:(cc + 1) * 128],
                                           rhs=hT[:, k, q5 * 512:(q5 + 1) * 512], start=(k == 0), stop=(k == 15))
                        return ins
                    P.op("pe", fn, reads=[wb] + B_hT[q5 * 4:(q5 + 1) * 4], writes=[PSB[bk]])
                    dst = foT[:, gb, cc, q5 * 512:(q5 + 1) * 512]
                    if (cc + q5) % 2 == 0:
                        P.op("act", lambda e, dst=dst, bk=bk: e.activation(out=dst, in_=psv(bk, [128, 512]), func=AF.Copy),
                             reads=[PSB[bk]], wacc=[B_foT[gb]])
                    else:
                        P.op("dve", lambda e, dst=dst, bk=bk: e.tensor_copy(out=dst, in_=psv(bk, [128, 512])),
                             reads=[PSB[bk]], wacc=[B_foT[gb]])
            ring.release(("fo", pidx, g))
            for tl in range(ntp):
                bk = nb("b", (4, 5))

                def fn(e, tl=tl, bk=bk):
                    ins = None
                    for cc in range(2):
                        ins = e.matmul(psv(bk, [128, 512]), lhsT=foT[:, gb, cc, tl * 128:(tl + 1) * 128], rhs=cs_t[:, cc, :],
                                       start=(cc == 0), stop=(cc == 1))
                    return ins
                P.op("pe", fn, reads=[B_foT[gb], B_const], writes=[PSB[bk]])
                dst = Yt[:, gb, tl, :]
                if tl % 2 == 0:
                    P.op("dve", lambda e, dst=dst, bk=bk: e.tensor_copy(out=dst, in_=psv(bk, [128, 512])),
                         reads=[PSB[bk]], wacc=[B_Y[gb]])
                else:
                    P.op("act", lambda e, dst=dst, bk=bk: e.activation(out=dst, in_=psv(bk, [128, 512]), func=AF.Copy),
                         reads=[PSB[bk]], wacc=[B_Y[gb]])
            for q5 in range(n5):
                slot, tbuf = ring.acquire(("tab", pidx, g, q5))
                tab = ring.slot_view(slot, [128, 2, ntp, 512], BF16)
                for cc in range(2):
                    bk = nb("c", (6, 7))

                    def fn(e, cc=cc, bk=bk, tab=tab):
                        ins = None
                        n = 0
                        for kt in range(ntp):
                            for sc_ in range(2):
                                ins = e.matmul(psv(bk, [128, 512]),
                                               lhsT=Yt[:, gb, kt, sc_ * 256 + cc * 128: sc_ * 256 + (cc + 1) * 128],
                                               rhs=tab[:, sc_, kt, :], start=(n == 0), stop=(n == 2 * ntp - 1))
                                n += 1
                        return ins
                    P.op("pe", fn, reads=[B_Y[gb], tbuf], writes=[PSB[bk]])
                    dst = fT[:, 2 * g + cc, q5 * 512:(q5 + 1) * 512]
                    if cc == 0:
                        P.op("act", lambda e, dst=dst, bk=bk: e.activation(out=dst, in_=psv(bk, [128, 512]), func=AF.Copy),
                             reads=[PSB[bk]], wacc=[B_fT[q5]])
                    else:
                        P.op("dve", lambda e, dst=dst, bk=bk: e.tensor_copy(out=dst, in_=psv(bk, [128, 512])),
                             reads=[PSB[bk]], wacc=[B_fT[q5]])
                ring.release(("tab", pidx, g, q5))
        for g in range(4):
            do_group(g)

        P.mark(f'p{pidx}.fnet')
        if pidx == 1:
            dump_fm("oTB", oT, 8, T, B_oT)
            dump_fm("fTB", fT, 8, T, B_fT)
        P.barrier()
        mo = tb + 16 * T
        merged = AR.view(mo, [128, 16, T], BF16)
        gsb = AR.view(mo + 32 * T, [128, 2, 2, 512], F32)
        t12 = AR.view(mo + 32 * T + 8192, [128, 2, 2, 512], F32)
        B_mg = [Buf() for _ in range(ntp)]
        B_gs, B_t12 = [[Buf(), Buf()], [Buf(), Buf()]], [[Buf(), Buf()], [Buf(), Buf()]]
        step = [0]

        def do_jp(jp):
            slot, wgb = ring.acquire(("gate", pidx, jp))
            wg = ring.slot_view(slot, [128, 2, 16, 256], BF16)
            slot2, wbb = ring.acquire2(("br", pidx, jp))
            wbr = ring.slot_view(slot2, [128, 2, 8, 256], BF16)
            def do_step(jj, q5):
                jch = 2 * jp + jj
                db = step[0] % 2
                step[0] += 1
                c5 = slice(q5 * 512, (q5 + 1) * 512)
                for ab in range(2):
                    bk = 4 * db + ab

                    def fn(e, ab=ab, bk=bk):
                        ins = None
                        for k in range(16):
                            ins = e.matmul(psv(bk, [128, 512]), lhsT=wg[:, ab, k, jj * 128:(jj + 1) * 128], rhs=hT[:, k, c5],
                                           start=(k == 0), stop=(k == 15))
                        return ins
                    P.op("pe", fn, reads=[wgb] + B_hT[q5 * 4:(q5 + 1) * 4], writes=[PSB[bk]])
                    P.op("act", lambda e, ab=ab, bk=bk: e.activation(out=gsb[:, db, ab, :], in_=psv(bk, [128, 512]), func=AF.Sigmoid,
                                                                   bias=bgatel_t[:, ab * 16 + jch: ab * 16 + jch + 1]),
                         reads=[PSB[bk], B_const], writes=[B_gs[db][ab]])
                if jj == 1 and q5 == n5 - 1:
                    ring.release(("gate", pidx, jp))
                for ab in range(2):
                    bk = 4 * db + 2 + ab
                    src_ = oT if ab == 0 else fT
                    srcb = (B_oT if ab == 0 else B_fT)

                    def fn(e, ab=ab, bk=bk, src_=src_):
                        ins = None
                        for k in range(8):
                            ins = e.matmul(psv(bk, [128, 512]), lhsT=wbr[:, ab, k, jj * 128:(jj + 1) * 128], rhs=src_[:, k, c5],
                                           start=(k == 0), stop=(k == 7))
                        return ins
                    rb = B_oT[q5 * 4:(q5 + 1) * 4] if ab == 0 else [B_fT[q5]]
                    P.op("pe", fn, reads=[wbb] + rb, writes=[PSB[bk]])
                    P.op("dve", lambda e, ab=ab, bk=bk: e.tensor_tensor(out=t12[:, db, ab, :], in0=psv(bk, [128, 512]), in1=gsb[:, db, ab, :],
                                                                      op=ALU.mult), reads=[PSB[bk], B_gs[db][ab]], writes=[B_t12[db][ab]])
                P.op("dve", lambda e: e.tensor_tensor(out=merged[:, jch, c5], in0=t12[:, db, 0, :], in1=t12[:, db, 1, :], op=ALU.add),
                     reads=[B_t12[db][0], B_t12[db][1]], wacc=B_mg[q5 * 4:(q5 + 1) * 4])

            for jj in range(2):
                for q5 in range(n5):
                    do_step(jj, q5)
            ring.release(("br", pidx, jp))
        for jp in range(8):
            do_jp(jp)

        P.mark(f'p{pidx}.merge')
        if pidx == 1:
            dump_fm("mgB", merged, 16, T, B_mg)
        P.barrier()
        mt = AR.view(O_BIG, [128, ntp, D], F32)
        xb2 = AR.view(mo + 32 * T, [128, 2, D], F32)
        B_m = [Buf() for _ in range(ntp)]
        B_xb2 = [Buf(), Buf()]
        B_st1 = Buf()
        junk = AR.view(mo + 32 * T + 16 * KB, [128, D], BF16)
        B_junk = Buf()
        B_pp = [Buf() for _ in range(ntp)]
        for c in range(4):
            slot, wob = ring.acquire(("wo", pidx, c))
            wo = ring.slot_view(slot, [128, 16, 512], BF16)
            for tl in range(ntp):
                bk = (c * ntp + tl) % 8

                def fn(e, tl=tl, bk=bk, wo=wo):
                    ins = None
                    for k in range(16):
                        ins = e.matmul(psv(bk, [128, 512]), lhsT=merged[:, k, tl * 128:(tl + 1) * 128], rhs=wo[:, k, :],
                                       start=(k == 0), stop=(k == 15))
                    return ins
                P.op("pe", fn, reads=[wob, B_mg[tl]], writes=[PSB[bk]])
                dst = mt[:, tl, c * 512:(c + 1) * 512]
                pcol = (t0 + tl) * 4 + c
                P.op("act", lambda e, bk=bk, pcol=pcol: e.activation(out=junk[:, 0:512], in_=psv(bk, [128, 512]), func=AF.Square,
                                                                   accum_out=part_t[:, pcol:pcol + 1]),
                     reads=[PSB[bk], B_part], writes=[B_junk], wacc=[B_pp[tl]])
                P.op("dve", lambda e, dst=dst, bk=bk, c=c: e.tensor_tensor(out=dst, in0=psv(bk, [128, 512]),
                                                                         in1=gag_t[:, j, c * 512:(c + 1) * 512], op=ALU.mult),
                     reads=[PSB[bk], B_gag], wacc=[B_m[tl]])
            ring.release(("wo", pidx, c))
        P.mark(f'p{pidx}.wout')
        pv = part_t[:, t0 * 4:(t0 + ntp) * 4].rearrange("p (t c) -> p t c", c=4)
        P.op("dve", lambda e: e.tensor_tensor(out=stat_t[:, 144 + t0:144 + t0 + ntp], in0=pv[:, :, 0], in1=pv[:, :, 1], op=ALU.add),
             reads=B_pp, writes=[B_st1])
        P.op("dve", lambda e: e.tensor_tensor(out=stat_t[:, 144 + t0:144 + t0 + ntp], in0=stat_t[:, 144 + t0:144 + t0 + ntp], in1=pv[:, :, 2], op=ALU.add),
             writes=[B_st1])
        P.op("dve", lambda e: e.tensor_tensor(out=stat_t[:, 144 + t0:144 + t0 + ntp], in0=stat_t[:, 144 + t0:144 + t0 + ntp], in1=pv[:, :, 3], op=ALU.add),
             writes=[B_st1])
        c0_, c1_ = 144 + t0, 160 + t0
        P.op("dve", lambda e: e.tensor_scalar(out=stat_t[:, c1_:c1_ + ntp], in0=stat_t[:, c0_:c0_ + ntp], scalar1=1.0 / D, scalar2=EPS,
                                              op0=ALU.mult, op1=ALU.add), reads=[B_st1], writes=[B_st1])
        P.op("dve", lambda e: e.reciprocal(out=stat_t[:, c1_:c1_ + ntp], in_=stat_t[:, c1_:c1_ + ntp]), writes=[B_st1])
        P.op("act", lambda e: e.activation(out=stat_t[:, c1_:c1_ + ntp], in_=stat_t[:, c1_:c1_ + ntp], func=AF.Sqrt), writes=[B_st1])
        xb4 = [xb2[:, 0, :], xb2[:, 1, :], AR.view(mo, [128, D], F32), AR.view(mo + 8 * KB, [128, D], F32)]
        B_xb4 = [B_xb2[0], B_xb2[1], Buf(), Buf()]
        sem_x4 = sem_x + [sem_xr]
        NBX = 4

        def xload(tn):
            bx = tn % NBX
            P.dma("sp", xb4[bx], DI("xin")[(t0 + tn) * 128:(t0 + tn + 1) * 128, :], sem_x4[bx], writes=[B_xb4[bx]],
                  extra=(B_st1.w if bx >= 2 else ()))
        for tn in range(min(NBX, ntp)):
            xload(tn)
        for it in range(ntp + 1):
            if it >= 1:
                tl = it - 1
                t = t0 + tl
                bx = tl % NBX
                P.op("dve", lambda e, tl=tl, bx=bx: e.scalar_tensor_tensor(out=mt[:, tl, :], in0=mt[:, tl, :], scalar=stat_t[:, c1_ + tl:c1_ + tl + 1],
                                                                         in1=xb4[bx], op0=ALU.mult, op1=ALU.add),
                     reads=[B_xb4[bx], B_st1], writes=[B_m[tl]])
                if tl + NBX < ntp:
                    xload(tl + NBX)
                P.op("act", lambda e, tl=tl, t=t: e.activation(out=junk, in_=mt[:, tl, :], func=AF.Square, accum_out=stat_t[:, 176 + t:177 + t]),
                     reads=[B_m[tl]], writes=[B_junk], wacc=[B_st2])
                x1_evs.append(P.dma("sp", x1s[t * 128:(t + 1) * 128, :], mt[:, tl, :], sem_x1, reads=[B_m[tl]]))
        P.mark(f'p{pidx}.resid')
        if "x1" in dbg and pidx == 0:
            for tl in range(ntp):
                fin.append(P.dma("sp", dbg["x1"][tl * 128:(tl + 1) * 128, :], mt[:, tl, :], sem_o, reads=[B_m[tl]]))
        return L, B_hT, B_oT

    B_t = Buf("dbgtmp")

    def dump_fm(name, ap, nchunk, T_, bufs):
        if name not in dbg:
            return
        tmpf = AR.view(AR.nbytes - 4 * KB, [128, 1024], F32)
        for k in range(nchunk):
            P.op("dve", lambda e, k=k: e.tensor_copy(out=tmpf[:, 0:T_], in_=ap[:, k, :]), reads=bufs, writes=[B_t])
            fin.append(P.dma("sp", dbg[name][:, k, :], tmpf[:, 0:T_], sem_o, reads=[B_t]))
    for pidx in range(2 if int(os.environ.get("GLA_STOP", "99")) >= 99 else 1):
        L, B_hT, B_oT = run_pass(pidx)
        if pidx == 0:
            if "hT" in dbg:
                tmpf = AR.view(AR.nbytes - 4 * KB, [128, 1024], F32)
                for k in range(16):
                    P.op("dve", lambda e, k=k, L=L, tmpf=tmpf: e.tensor_copy(out=tmpf, in_=L["hT"][:, k, :]), reads=B_hT, writes=[B_t])
                    fin.append(P.dma("sp", dbg["hT"][:, k, :], tmpf, sem_o, reads=[B_t]))
            if "oT" in dbg:
                tmpf = AR.view(AR.nbytes - 4 * KB, [128, 1024], F32)
                for k in range(8):
                    P.op("dve", lambda e, k=k, L=L, tmpf=tmpf: e.tensor_copy(out=tmpf, in_=L["oT"][:, k, :]), reads=B_oT, writes=[B_t])
                    fin.append(P.dma("sp", dbg["oT"][:, k, :], tmpf, sem_o, reads=[B_t]))
        if stage < 3:
            pass


    if stage >= 4:
        P.op("dve", lambda e: e.tensor_scalar(out=stat_t[:, 192:204], in0=stat_t[:, 176:188], scalar1=1.0 / D, scalar2=EPS,
                                              op0=ALU.mult, op1=ALU.add), reads=[B_st2], writes=[B_st2])
        P.op("dve", lambda e: e.reciprocal(out=stat_t[:, 192:204], in_=stat_t[:, 192:204]), writes=[B_st2])
        P.op("act", lambda e: e.activation(out=stat_t[:, 192:204], in_=stat_t[:, 192:204], func=AF.Sqrt), writes=[B_st2])
        P.barrier()
        mods_bcast(1)
        P.mark('mods1')
        TB = 768
        aT = AR.view(O_BIG, [128, 44, TB], BF16)
        R0 = O_BIG + 44 * TB * 2
        h2T = AR.view(R0, [128, 16, TB], BF16)
        x1p = AR.view(O_BIG + 48 * KB, [128, 2, D], F32)
        B_x1p_g = [Buf(), Buf()]
        xn2 = AR.view(R0 + 49152, [128, 2, D], BF16)
        sgt2 = AR.view(R0 + 49152, [128, 2, TB], F32)
        tmpb = AR.view(R0 + 57344, [128, 2, 8, 128], F32)
        BLK = {}
        yb = AR.view(R0, [128, 6, D], F32)
        x1f = AR.view(R0 + 49152, [128, 2, D], F32)
        junk2 = AR.view(O_BIG, [128, D], BF16)
        sem_p = [P.dsem("x1p0"), P.dsem("x1p1")]

        def ffn_block(fb):
            B_h2 = [Buf() for _ in range(6)]
            B_x1p, B_xn2 = B_x1p_g, [Buf(), Buf()]
            B_aT = Buf()
            B_sg = [Buf(), Buf()]
            B_y = [Buf() for _ in range(6)]
            B_x1f = [Buf(), Buf()]
            B_st3, B_j2 = Buf(), Buf()
            B_p3 = [Buf() for _ in range(6)]
            junk3 = AR.view(R0 + 49152, [128, 512], BF16)
            B_tmpb = [Buf(), Buf()]
            for tl in range(6):
                t = fb * 6 + tl
                b = tl % 2
                jc = 0 if t < 8 else 1
                if not (fb == 1 and tl < 2):
                    P.dma("sp", x1p[:, b, :], x1s[t * 128:(t + 1) * 128, :], sem_p[b], writes=[B_x1p[b]])
                P.op("act", lambda e, b=b, t=t: e.activation(out=xn2[:, b, :], in_=x1p[:, b, :], func=AF.Copy,
                                                           scale=stat_t[:, 192 + t:193 + t]), reads=[B_x1p[b], B_st2], writes=[B_xn2[b]])
                for half in range(2):
                    bank = 6 + half

                    def fn(e, half=half, b=b, bank=bank):
                        ins = None
                        for kk in range(8):
                            kc = half * 8 + kk
                            ins = e.transpose(out=psv(bank, [128, 8, 128], BF16)[:, kk, :],
                                              in_=xn2[:, b, kc * 128:(kc + 1) * 128], identity=ident_t)
                        return ins
                    P.op("pe", fn, reads=[B_xn2[b], B_const], writes=[PSB[bank]])
                    h8 = half * 8
                    P.op("dve", lambda e, half=half, bank=bank, h8=h8, jc=jc: e.tensor_tensor(
                        out=tmpb[:, half, :, :], in0=psv(bank, [128, 8, 128], BF16),
                        in1=modF_t[:, 3, h8:h8 + 8, jc:jc + 1].to_broadcast([128, 8, 128]), op=ALU.mult),
                        reads=[PSB[bank], B_modF], writes=[B_tmpb[half]])
                    P.op("dve", lambda e, half=half, h8=h8, jc=jc, tl=tl: e.tensor_tensor(
                        out=h2T[:, h8:h8 + 8, tl * 128:(tl + 1) * 128], in0=tmpb[:, half, :, :],
                        in1=modF_t[:, 2, h8:h8 + 8, jc:jc + 1].to_broadcast([128, 8, 128]), op=ALU.add),
                        reads=[B_tmpb[half], B_modF], wacc=[B_h2[tl]], extra=(BLK.get("st0", [])[0:3] if fb == 1 else ()))
            P.mark(f'f{fb}.prep')
            P.barrier()
            def do_hp(hp):
                slot, wb = ring.acquire(("f1", fb, hp))
                wgu = ring.slot_view(slot, [128, 2, 16, 256], BF16)
                for hh in range(2):
                    hc = 2 * hp + hh
                    bs = 4 * (hc % 2)

                    def fn(e, hh=hh, bs=bs):
                        ins = None
                        for gu in range(2):
                            for k in range(16):
                                lw = wgu[:, gu, k, hh * 128:(hh + 1) * 128]
                                e.matmul(psv(bs + 2 * gu, [128, 512]), lhsT=lw, rhs=h2T[:, k, 0:512], start=(k == 0), stop=(k == 15))
                                ins = e.matmul(psv(bs + 2 * gu + 1, [128, 512])[:, 0:256], lhsT=lw, rhs=h2T[:, k, 512:768],
                                               start=(k == 0), stop=(k == 15))
                        return ins
                    P.op("pe", fn, reads=[wb] + B_h2, writes=[PSB[bs], PSB[bs + 1], PSB[bs + 2], PSB[bs + 3]])
                    sb_ = hc % 2
                    P.op("act", lambda e, bs=bs, sb_=sb_: e.activation(out=sgt2[:, sb_, 0:512], in_=psv(bs, [128, 512]), func=AF.Silu),
                         reads=[PSB[bs]], writes=[B_sg[sb_]])
                    P.op("act", lambda e, bs=bs, sb_=sb_: e.activation(out=sgt2[:, sb_, 512:768], in_=psv(bs + 1, [128, 512])[:, 0:256], func=AF.Silu),
                         reads=[PSB[bs + 1]], wacc=[B_sg[sb_]])
                    P.op("dve", lambda e, bs=bs, sb_=sb_, hc=hc: e.tensor_tensor(out=aT[:, hc, 0:512], in0=psv(bs + 2, [128, 512]),
                                                                               in1=sgt2[:, sb_, 0:512], op=ALU.mult),
                         reads=[PSB[bs + 2], B_sg[sb_]], wacc=[B_aT])
                    P.op("dve", lambda e, bs=bs, sb_=sb_, hc=hc: e.tensor_tensor(out=aT[:, hc, 512:768], in0=psv(bs + 3, [128, 512])[:, 0:256],
                                                                               in1=sgt2[:, sb_, 512:768], op=ALU.mult),
                         reads=[PSB[bs + 3], B_sg[sb_]], wacc=[B_aT])
                ring.release(("f1", fb, hp))
            for hp in range(22):
                do_hp(hp)
            P.mark(f'f{fb}.F1')
            for c in range(4):
                for pz in range(4):
                    slot, wb = ring.acquire(("f2", fb, c, pz))
                    wd = ring.slot_view(slot, [128, 11, 512], BF16)
                    for tl in range(6):
                        def fn(e, tl=tl, pz=pz, wd=wd):
                            ins = None
                            for kk in range(11):
                                ins = e.matmul(psv(tl, [128, 512]), lhsT=aT[:, pz * 11 + kk, tl * 128:(tl + 1) * 128], rhs=wd[:, kk, :],
                                               start=(pz == 0 and kk == 0), stop=(pz == 3 and kk == 10))
                            return ins
                        if pz == 0:
                            P.op("pe", fn, reads=[wb, B_aT], writes=[PSB[tl]])
                        else:
                            P.op("pe", fn, reads=[wb, B_aT], wacc=[PSB[tl]])
                    ring.release(("f2", fb, c, pz))
                for tl in range(6):
                    dst = yb[:, tl, c * 512:(c + 1) * 512]
                    t = fb * 6 + tl
                    jc = 0 if t < 8 else 1
                    pcol = 48 + t * 4 + c
                    P.op("act", lambda e, tl=tl, pcol=pcol: e.activation(out=junk3, in_=psv(tl, [128, 512]), func=AF.Square,
                                                                       accum_out=part_t[:, pcol:pcol + 1]),
                         reads=[PSB[tl], B_part], writes=[B_j2], wacc=[B_p3[tl]])
                    P.op("dve", lambda e, dst=dst, tl=tl, jc=jc, c=c: e.tensor_tensor(out=dst, in0=psv(tl, [128, 512]),
                                                                                   in1=gag_t[:, jc, c * 512:(c + 1) * 512], op=ALU.mult),
                         reads=[PSB[tl], B_gag], wacc=[B_y[tl]])
            P.mark(f'f{fb}.F2')
            x1f6 = AR.view(O_BIG, [128, 6, D], F32)
            B_x1f6 = [Buf() for _ in range(6)]
            aT_dead = list(B_p3[5].w) + list(B_y[5].w)
            for tl in range(6):
                P.dma("sp", x1f6[:, tl, :], x1s[(fb * 6 + tl) * 128:(fb * 6 + tl + 1) * 128, :], sem_f[tl], writes=[B_x1f6[tl]],
                      extra=aT_dead)
            pv3 = part_t[:, 48 + fb * 24:48 + fb * 24 + 24].rearrange("p (t c) -> p t c", c=4)
            s3 = stat_t[:, 208 + fb * 6:208 + fb * 6 + 6]
            P.op("dve", lambda e: e.tensor_tensor(out=s3, in0=pv3[:, :, 0], in1=pv3[:, :, 1], op=ALU.add), reads=B_p3, writes=[B_st3])
            P.op("dve", lambda e: e.tensor_tensor(out=s3, in0=s3, in1=pv3[:, :, 2], op=ALU.add), writes=[B_st3])
            P.op("dve", lambda e: e.tensor_tensor(out=s3, in0=s3, in1=pv3[:, :, 3], op=ALU.add), writes=[B_st3])
            c0_, c1_ = 208 + fb * 6, 224 + fb * 6
            P.op("dve", lambda e: e.tensor_scalar(out=stat_t[:, c1_:c1_ + 6], in0=stat_t[:, c0_:c0_ + 6], scalar1=1.0 / D, scalar2=EPS,
                                                  op0=ALU.mult, op1=ALU.add), reads=[B_st3], writes=[B_st3])
            P.op("dve", lambda e: e.reciprocal(out=stat_t[:, c1_:c1_ + 6], in_=stat_t[:, c1_:c1_ + 6]), writes=[B_st3])
            P.op("act", lambda e: e.activation(out=stat_t[:, c1_:c1_ + 6], in_=stat_t[:, c1_:c1_ + 6], func=AF.Sqrt), writes=[B_st3])
            for it in range(7):
                if it >= 1:
                    tl = it - 1
                    t = fb * 6 + tl
                    P.op("dve", lambda e, tl=tl: e.scalar_tensor_tensor(out=yb[:, tl, :], in0=yb[:, tl, :], scalar=stat_t[:, c1_ + tl:c1_ + tl + 1],
                                                                       in1=x1f6[:, tl, :], op0=ALU.mult, op1=ALU.add),
                         reads=[B_x1f6[tl], B_st3], writes=[B_y[tl]])
                    ev_st = P.dma("sp", yout[t * 128:(t + 1) * 128, :], yb[:, tl, :], sem_os[tl], reads=[B_y[tl]])
                    out_evs.append(ev_st)
                    BLK.setdefault(f"st{fb}", []).append(ev_st)
            P.mark(f'f{fb}.final')
            if fb == 0:
                for b_ in range(2):
                    P.dma("sp", x1p[:, b_, :], x1s[(6 + b_) * 128:(7 + b_) * 128, :], sem_p[b_], writes=[B_x1p_g[b_]])
                P.barrier(skip=sem_os + sem_p)
        for fb in range(2):
            ffn_block(fb)

    if "modF" in dbg:
        fin.append(P.dma("sp", dbg["modF"], modF_t.rearrange("p a b c -> p (a b c)"), sem_o, reads=[B_modF]))
    if "gag" in dbg:
        fin.append(P.dma("sp", dbg["gag"], gag_t.rearrange("p a b -> p (a b)"), sem_o, reads=[B_gag]))
    P.wait_only("sp", fin + out_evs)
    print("recorded ops", P.nops, {k: len(v_) for k, v_ in P.ops.items()})
    if os.environ.get("KERNEL_MARKS"):
        import json as _json
        _json.dump(P.marks, open(os.environ["KERNEL_MARKS"], "w"))
    P.emit()
    return nc, list(dts.keys())


def _dft_tables(T, blocks):
    n = T // blocks
    idx = np.arange(n)
    ang = 2.0 * np.pi * ((idx[:, None] * idx[None, :]) % n) / n
    c = np.cos(ang) / np.sqrt(n)
    s = -np.sin(ang) / np.sqrt(n)
    ct = np.zeros((T, T), np.float32)
    st = np.zeros((T, T), np.float32)
    for b in range(blocks):
        ct[b * n:(b + 1) * n, b * n:(b + 1) * n] = c
        st[b * n:(b + 1) * n, b * n:(b + 1) * n] = s
    return ct.astype(ml_dtypes.bfloat16), st.astype(ml_dtypes.bfloat16)


def _consts():
    bf = ml_dtypes.bfloat16
    i = np.arange(128)
    same = (i[:, None] // 64) == (i[None, :] // 64)
    Mf = (same & (i[:, None] <= i[None, :])).astype(np.float32)
    Mb = (same & (i[:, None] >= i[None, :])).astype(np.float32)
    I = np.eye(128, dtype=np.float32)
    cum = np.zeros((128, 2, 256), np.float32)
    cum[:, 0, :128] = -Mf / 16.0
    cum[:, 0, 128:] = -(Mb - I) / 16.0
    cum[:, 1, :128] = -Mb / 16.0
    cum[:, 1, 128:] = -(Mf - I) / 16.0
    amask = np.stack([Mf, Mb], axis=1)
    c = np.arange(256)
    ang = 2.0 * np.pi * ((c[:, None] * c[None, :]) % 256) / 256.0
    cs = np.concatenate([np.cos(ang) / 16.0, np.sin(ang) / 16.0], axis=1).astype(bf)
    return dict(cum=cum, amask=np.ascontiguousarray(amask), cs=cs, ident=np.eye(128, dtype=np.float32).astype(bf))


def _fm(v, n):
    return np.ascontiguousarray(np.asarray(v, np.float32).reshape(n, 128).T)


def make_in_maps(inp):
    g = lambda k: np.asarray(inp[k])
    x_prompt, x_sample = g("x_prompt"), g("x_sample")
    c, c_ctx = g("c"), g("c_ctx")
    sf, sb = g("state_gla_fwd"), g("state_gla_bwd")
    consts = _consts()
    ctA1, nstA1 = _dft_tables(1024, 1)
    ctA4, nstA4 = _dft_tables(1024, 4)
    ctB, nstB = _dft_tables(512, 2)
    a_fb = np.zeros((33, 2, 512), np.float32)
    a_fb[0:16, 0] = g("w_a2_fwd")[0]
    a_fb[32, 0] = g("b_a_fwd")[0]
    a_fb[16:32, 1] = g("w_a2_bwd")[0]
    a_fb[32, 1] = g("b_a_bwd")[0]
    shared = dict(
        ctB=ctB, nstB=nstB, **consts,
        w_ada=g("w_ada")[0], b_adal=_fm(g("b_ada")[0], 96), b_ada=g("b_ada"),
        gprel=np.concatenate([_fm(g("norm_pre_mix")[0], 16), _fm(g("norm_pre_ffn")[0], 16)], axis=1),
        gpost=np.stack([g("norm_post_mix")[0], g("norm_post_ffn")[0]], axis=0),
        w_in=g("w_in")[0], a_fb=a_fb, gn=g("gla_out_norm").reshape(1, 1024),
        w_br_gla=g("w_br_gla")[0], w_br_four=g("w_br_four")[0], w_gate=g("w_gate")[0],
        b_gatel=_fm(g("b_gate")[0], 32), w_out=g("w_out")[0],
        w_ffn_gate=g("w_ffn_gate")[0], w_ffn_up=g("w_ffn_up")[0], w_ffn_down=g("w_ffn_down")[0],
    )
    maps = []
    plan = []
    zero_state = np.zeros((4, 128, 256), np.float32)
    for core in range(8):
        if core < 4:
            big = x_sample[core]
            pr = [2 * core, 2 * core + 1]
            conds = np.stack([c[core], c_ctx], axis=0)
            s0f_, s0b_ = sf[core, 0], sb[core, 0]
            lk = 1.0
            cta, nsta = ctA1, nstA1
            segs = [None, None, None, None, pr[0], pr[1]]
        else:
            base = 8 + 6 * (core - 4)
            big = x_prompt[base:base + 4].reshape(1024, D)
            pr = [base + 4, base + 5]
            conds = np.stack([c_ctx, c_ctx], axis=0)
            s0f_, s0b_ = zero_state, zero_state
            lk = 0.0
            cta, nsta = ctA4, nstA4
            segs = [base, base + 1, base + 2, base + 3, base + 4, base + 5]
        xin = np.concatenate([big, x_prompt[pr[0]], x_prompt[pr[1]]], axis=0)
        condl = np.zeros((128, 32), np.float32)
        for j in range(2):
            condl[:, j::2] = _fm(conds[j], 16)
        linkv = np.zeros((128, 2), np.float32)
        linkv[:, 0] = lk
        m = dict(shared)
        m.update(xin=np.ascontiguousarray(xin, dtype=np.float32), condl=condl,
                 s0f=np.ascontiguousarray(s0f_, dtype=np.float32), s0b=np.ascontiguousarray(s0b_, dtype=np.float32),
                 link=linkv, ctA=cta, nstA=nsta)
        maps.append(m)
        plan.append(segs)
    return maps, plan


_NC_CACHE = {}


def kernel(**inputs):
    maps, plan = make_in_maps(inputs)
    if "nc" not in _NC_CACHE:
        _NC_CACHE["nc"] = build_nc()
    nc, names = _NC_CACHE["nc"]
    maps = [{k: m[k] for k in names} for m in maps]
    res = bass_utils.run_bass_kernel_spmd(nc, maps, core_ids=list(range(8)))
    y_prompt = np.zeros((32, 256, D), np.float32)
    y_sample = np.zeros((4, 1024, D), np.float32)
    nsf = np.zeros((32, 1, 4, 128, 256), np.float32)
    nsb = np.zeros((32, 1, 4, 128, 256), np.float32)
    for core in range(8):
        r = res.results[core]
        yo = r["yout"]
        segs = plan[core]
        if core < 4:
            y_sample[core] = yo[0:1024]
        for s, pid in enumerate(segs):
            if pid is None:
                continue
            y_prompt[pid] = yo[s * 256:(s + 1) * 256]
            nsf[pid, 0] = r["snf"][s]
            nsb[pid, 0] = r["snb"][s]
    return (y_prompt, y_sample, nsf, nsb)
```
